# Optimizing a Trainium2 kernel written in Bass

```python
import math
import jax, jax.numpy as jnp
from jax import lax
import numpy as np

D_MODEL = 1024
BATCH = 8
SEQ = 8192
DEPTH = 4

N_MIXERS = 4
D_FF = 2816
EPS = 1e-6
MASK_VALUE = -1e30

DIL_PATTERNS = ((128, 1), (512, 4), (2048, 16))
N_GROUPS_A = len(DIL_PATTERNS)
HEADS_PER_GROUP = 8
HEAD_DIM_A = 64
ATTN_WIDTH = HEADS_PER_GROUP * HEAD_DIM_A
BAND_BLOCK = 64

CONV_WIDTH = 31

FOURIER_GROUPS = 4
FOURIER_GROUP_DIM = D_MODEL // FOURIER_GROUPS

GLA_HEADS = 4
GLA_DK = D_MODEL // 2
GLA_DV = D_MODEL
GLA_HEAD_K = GLA_DK // GLA_HEADS
GLA_HEAD_V = GLA_DV // GLA_HEADS
GLA_GATE_RANK = 16
GLA_TAU = 16.0
GLA_CHUNK = 64

N_ATTN_LAYERS = (DEPTH + 3) // 4
N_CONV_LAYERS = (DEPTH + 2) // 4
N_FOURIER_LAYERS = (DEPTH + 1) // 4
N_GLA_LAYERS = DEPTH // 4

kernel_name = "hybrid_interleaved_bidir_encoder"

F32 = jnp.float32


def _rmsnorm(x, g):
    xf = x.astype(F32)
    y = xf * lax.rsqrt(jnp.mean(xf * xf, axis=-1, keepdims=True) + EPS)
    return (y * g.astype(F32)).astype(x.dtype)


def _swiglu(h, w1, w3, w2):
    return (jax.nn.silu(h @ w1) * (h @ w3)) @ w2


def _alibi_slopes():
    n = N_GROUPS_A * HEADS_PER_GROUP
    return jnp.exp2(-8.0 * jnp.arange(1, n + 1, dtype=F32) / n)


def _band_attention(q, k, v, radius, slopes_dist):
    n, L, h, dh = q.shape
    nb = -(-L // BAND_BLOCK)
    Lp = nb * BAND_BLOCK
    pad_end = Lp - L
    qp = jnp.pad(q, ((0, 0), (0, pad_end), (0, 0), (0, 0))).reshape(n, nb, BAND_BLOCK, h, dh)

    def windows(t):
        tp = jnp.pad(t, ((0, 0), (BAND_BLOCK, BAND_BLOCK + pad_end), (0, 0), (0, 0)))
        tp = tp.reshape(n, nb + 2, BAND_BLOCK, h, dh)
        return jnp.concatenate([tp[:, :-2], tp[:, 1:-1], tp[:, 2:]], axis=2)

    kw, vw = windows(k), windows(v)
    qi = jnp.arange(Lp).reshape(nb, BAND_BLOCK)
    kj = (jnp.arange(nb)[:, None] - 1) * BAND_BLOCK + jnp.arange(3 * BAND_BLOCK)[None, :]
    rel = kj[:, None, :] - qi[:, :, None]
    valid = (jnp.abs(rel) <= radius) & (kj[:, None, :] >= 0) & (kj[:, None, :] < L)
    s = jnp.einsum('nbqhd,nbkhd->nhbqk', qp.astype(F32), kw.astype(F32)) * (dh ** -0.5)
    s = s - slopes_dist[:, None, None, None] * jnp.abs(rel).astype(F32)
    s = jnp.where(valid, s, MASK_VALUE)
    lse = jax.nn.logsumexp(s, axis=-1)
    p = jnp.exp(s - lse[..., None])
    o = jnp.einsum('nhbqk,nbkhd->nbqhd', p, vw.astype(F32)).reshape(n, Lp, h, dh)[:, :L]
    lse = lse.transpose(0, 2, 3, 1).reshape(n, Lp, h)[:, :L]
    return o, lse


def _dilated_group(q, k, v, window, dilation, slopes):
    b, s, h, dh = q.shape
    sub = s // dilation

    def split(t):
        return t.reshape(b, sub, dilation, h, dh).transpose(0, 2, 1, 3, 4).reshape(b * dilation, sub, h, dh)

    o, lse = _band_attention(split(q), split(k), split(v), window // (2 * dilation), slopes * dilation)
    o = o.reshape(b, dilation, sub, h, dh).transpose(0, 2, 1, 3, 4).reshape(b, s, h, dh)
    lse = lse.reshape(b, dilation, sub, h).transpose(0, 2, 1, 3).reshape(b, s, h)
    return o, lse


def _dilated_attention_mixer(u, w_qkv, w_o):
    b, s, _ = u.shape
    qkv = (u @ w_qkv).reshape(b, s, N_GROUPS_A, 3, HEADS_PER_GROUP, HEAD_DIM_A)
    slopes = _alibi_slopes()
    outs, lses = [], []
    for g, (win, dil) in enumerate(DIL_PATTERNS):
        o, l = _dilated_group(qkv[:, :, g, 0], qkv[:, :, g, 1], qkv[:, :, g, 2], win, dil,
                              slopes[g * HEADS_PER_GROUP:(g + 1) * HEADS_PER_GROUP])
        outs.append(o)
        lses.append(l)
    wts = jax.nn.softmax(jnp.stack(lses, 0), axis=0)
    o = jnp.einsum('gbsh,gbshd->bshd', wts, jnp.stack(outs, 0))
    return o.reshape(b, s, ATTN_WIDTH).astype(u.dtype) @ w_o


def _conv_module(u, w_pw1, b_pw1, w_dw, b_dw, ln_g, ln_b, w_pw2, b_pw2):
    z = u @ w_pw1 + b_pw1
    a, gate = jnp.split(z, 2, axis=-1)
    z = a * jax.nn.sigmoid(gate)
    z = lax.conv_general_dilated(z, w_dw[:, None, :].astype(z.dtype), window_strides=(1,),
                                 padding=[(CONV_WIDTH // 2, CONV_WIDTH // 2)],
                                 dimension_numbers=('NWC', 'WIO', 'NWC'),
                                 feature_group_count=D_MODEL) + b_dw
    zf = z.astype(F32)
    mu = jnp.mean(zf, axis=-1, keepdims=True)
    var = jnp.mean(jnp.square(zf - mu), axis=-1, keepdims=True)
    zf = (zf - mu) * lax.rsqrt(var + EPS) * ln_g.astype(F32) + ln_b.astype(F32)
    return jax.nn.silu(zf).astype(u.dtype) @ w_pw2 + b_pw2


def _fourier_mixer(u, w_f, b_f):
    b, s, _ = u.shape
    uf = u.astype(F32).reshape(b, s, FOURIER_GROUPS, FOURIER_GROUP_DIM)
    mixed = jnp.fft.fft2(uf, axes=(1, 3), norm='ortho').real
    return mixed.reshape(b, s, D_MODEL).astype(u.dtype) @ w_f + b_f


def _gla_direction(q, k, v, log_a, include_diag):
    b, s, H, dk = q.shape
    dv = v.shape[-1]
    n = s // GLA_CHUNK

    def chunk(t):
        return t.reshape(b, n, GLA_CHUNK, H, t.shape[-1]).astype(F32)

    qc, kc, vc, gc = chunk(q), chunk(k), chunk(v), chunk(log_a)
    cum = jnp.cumsum(gc, axis=2)
    ref = cum[:, :, GLA_CHUNK // 2 - 1:GLA_CHUNK // 2]
    scores = jnp.einsum('bnihk,bnjhk->bnhij', qc * jnp.exp(cum - ref), kc * jnp.exp(ref - cum))
    mask = jnp.tril(jnp.ones((GLA_CHUNK, GLA_CHUNK), dtype=bool), k=0 if include_diag else -1)
    o_intra = jnp.einsum('bnhij,bnjhv->bnihv', jnp.where(mask, scores, 0.0), vc)
    last = cum[:, :, -1]
    q_inter = qc * jnp.exp(cum)
    k_state = kc * jnp.exp(last[:, :, None] - cum)

    def step(state, inp):
        q_i, k_s, v_c, dec = inp
        o = jnp.einsum('bihk,bhkv->bihv', q_i, state)
        state = state * jnp.exp(dec)[..., None] + jnp.einsum('bjhk,bjhv->bhkv', k_s, v_c)
        return state, o

    state0 = jnp.zeros((b, H, dk, dv), F32)
    xs = (jnp.moveaxis(q_inter, 1, 0), jnp.moveaxis(k_state, 1, 0),
          jnp.moveaxis(vc, 1, 0), jnp.moveaxis(last, 1, 0))
    _, o_inter = lax.scan(step, state0, xs)
    o = o_intra + jnp.moveaxis(o_inter, 0, 1)
    return o.reshape(b, s, H, dv)


def _gla_mixer(u, w_in, w_a1, w_a2, b_a, norm_g, w_o):
    b, s, _ = u.shape
    proj = u @ w_in
    q, k, v, r = jnp.split(proj, [GLA_DK, 2 * GLA_DK, 2 * GLA_DK + GLA_DV], axis=-1)
    q = q.reshape(b, s, GLA_HEADS, GLA_HEAD_K) * (GLA_HEAD_K ** -0.5)
    k = k.reshape(b, s, GLA_HEADS, GLA_HEAD_K)
    v = v.reshape(b, s, GLA_HEADS, GLA_HEAD_V)

    def log_decay(d):
        z = ((u @ w_a1[d]) @ w_a2[d] + b_a[d]).astype(F32)
        return (jax.nn.log_sigmoid(z) / GLA_TAU).reshape(b, s, GLA_HEADS, GLA_HEAD_K)

    o_fwd = _gla_direction(q, k, v, log_decay(0), True)
    o_bwd = _gla_direction(q[:, ::-1], k[:, ::-1], v[:, ::-1], log_decay(1)[:, ::-1], False)[:, ::-1]
    o = o_fwd + o_bwd
    o = o * lax.rsqrt(jnp.mean(o * o, axis=-1, keepdims=True) + EPS)
    o = o.reshape(b, s, GLA_DV) * norm_g.astype(F32) * jax.nn.silu(r.astype(F32))
    return o.astype(u.dtype) @ w_o


def setup_inputs(seed: int = 0) -> dict:
    key = jax.random.key(seed)
    ks = iter(jax.random.split(key, 32))

    def nrm(shape, fan_in):
        return jax.random.normal(next(ks), shape, F32) * (fan_in ** -0.5)

    def gain(shape):
        return 1.0 + 0.02 * jax.random.normal(next(ks), shape, F32)

    def bias(shape, scale=0.02):
        return scale * jax.random.normal(next(ks), shape, F32)

    D = D_MODEL
    return {
        "x": jax.random.normal(next(ks), (BATCH, SEQ, D), F32),
        "norm_g": gain((DEPTH, 3, D)),
        "final_norm_g": gain((D,)),
        "ffn_w1": nrm((DEPTH, 2, D, D_FF), D),
        "ffn_w3": nrm((DEPTH, 2, D, D_FF), D),
        "ffn_w2": nrm((DEPTH, 2, D_FF, D), D_FF),
        "attn_w_qkv": nrm((N_ATTN_LAYERS, D, N_GROUPS_A * 3 * ATTN_WIDTH), D),
        "attn_w_o": nrm((N_ATTN_LAYERS, ATTN_WIDTH, D), ATTN_WIDTH),
        "conv_w_pw1": nrm((N_CONV_LAYERS, D, 2 * D), D),
        "conv_b_pw1": bias((N_CONV_LAYERS, 2 * D)),
        "conv_w_dw": nrm((N_CONV_LAYERS, CONV_WIDTH, D), CONV_WIDTH),
        "conv_b_dw": bias((N_CONV_LAYERS, D)),
        "conv_ln_g": gain((N_CONV_LAYERS, D)),
        "conv_ln_b": bias((N_CONV_LAYERS, D)),
        "conv_w_pw2": nrm((N_CONV_LAYERS, D, D), D),
        "conv_b_pw2": bias((N_CONV_LAYERS, D)),
        "fnet_w": nrm((N_FOURIER_LAYERS, D, D), D),
        "fnet_b": bias((N_FOURIER_LAYERS, D)),
        "gla_w_in": nrm((N_GLA_LAYERS, D, 2 * GLA_DK + 2 * GLA_DV), D),
        "gla_w_a1": nrm((N_GLA_LAYERS, 2, D, GLA_GATE_RANK), D),
        "gla_w_a2": nrm((N_GLA_LAYERS, 2, GLA_GATE_RANK, GLA_DK), GLA_GATE_RANK),
        "gla_b_a": bias((N_GLA_LAYERS, 2, GLA_DK), 0.1),
        "gla_norm_g": gain((N_GLA_LAYERS, GLA_DV)),
        "gla_w_o": nrm((N_GLA_LAYERS, GLA_DV, D), GLA_DV),
    }


def reference(x, norm_g, final_norm_g, ffn_w1, ffn_w3, ffn_w2, attn_w_qkv, attn_w_o,
              conv_w_pw1, conv_b_pw1, conv_w_dw, conv_b_dw, conv_ln_g, conv_ln_b,
              conv_w_pw2, conv_b_pw2, fnet_w, fnet_b, gla_w_in, gla_w_a1, gla_w_a2,
              gla_b_a, gla_norm_g, gla_w_o):
    h = x
    for i in range(DEPTH):
        m, j = i % N_MIXERS, i // N_MIXERS
        h = h + 0.5 * _swiglu(_rmsnorm(h, norm_g[i, 0]), ffn_w1[i, 0], ffn_w3[i, 0], ffn_w2[i, 0])
        u = _rmsnorm(h, norm_g[i, 1])
        if m == 0:
            mix = _dilated_attention_mixer(u, attn_w_qkv[j], attn_w_o[j])
        elif m == 1:
            mix = _conv_module(u, conv_w_pw1[j], conv_b_pw1[j], conv_w_dw[j], conv_b_dw[j],
                               conv_ln_g[j], conv_ln_b[j], conv_w_pw2[j], conv_b_pw2[j])
        elif m == 2:
            mix = _fourier_mixer(u, fnet_w[j], fnet_b[j])
        else:
            mix = _gla_mixer(u, gla_w_in[j], gla_w_a1[j], gla_w_a2[j], gla_b_a[j],
                             gla_norm_g[j], gla_w_o[j])
        h = h + mix
        h = h + 0.5 * _swiglu(_rmsnorm(h, norm_g[i, 2]), ffn_w1[i, 1], ffn_w3[i, 1], ffn_w2[i, 1])
    return _rmsnorm(h, final_norm_g)
```

```python
import os
import math
import numpy as np
import concourse.bass as bass
import concourse.mybir as mybir
from concourse.bass_utils import run_bass_kernel_spmd

F32 = mybir.dt.float32
BF16 = mybir.dt.bfloat16
ALU = mybir.AluOpType
AF = mybir.ActivationFunctionType

D = 1024
S = 8192
FF = 2816
KC = 8
FC = 22
TT = 512
NT = S // TT
EPS = 1e-6
NCORES = 8
DIL = (1, 4, 16)
WIN = (128, 512, 2048)


class Res:
    __slots__ = ("lw", "rd")

    def __init__(self):
        self.lw = None
        self.rd = {}


class Stream:
    __slots__ = ("name", "sem", "cnt")

    def __init__(self, name, sem):
        self.name = name
        self.sem = sem
        self.cnt = 0


class Prog:
    ENG = ("pe", "act", "dve", "pool", "sp")

    def __init__(self, nc):
        self.nc = nc
        self.e = {"pe": nc.tensor, "act": nc.scalar, "dve": nc.vector, "pool": nc.gpsimd, "sp": nc.sync}
        self.es = {k: Stream(k, nc.alloc_semaphore(name="s_" + k)) for k in self.ENG}
        self.seen = {k: {} for k in self.ENG}
        self.dstreams = []
        self.nobar = set()
        self.ninst = {k: 0 for k in self.ENG}

    def res(self, n=None):
        if n is None:
            return Res()
        return [Res() for _ in range(n)]

    def dstream(self, name, barrier=True):
        s = Stream(name, self.nc.alloc_semaphore(name="d_" + name))
        self.dstreams.append(s)
        if not barrier:
            self.nobar.add(s)
        return s

    def _wait(self, eng, stream, val):
        if val <= 0 or self.seen[eng].get(stream, 0) >= val:
            return
        self.e[eng].wait_ge(stream.sem, val)
        self.ninst[eng] += 1
        self.seen[eng][stream] = val

    def _deps(self, eng, reads, writes):
        deps = {}

        def add(t):
            if t is not None and deps.get(t[0], 0) < t[1]:
                deps[t[0]] = t[1]
        for r in reads:
            add(r.lw)
        for w in writes:
            add(w.lw)
            for s, v in w.rd.items():
                add((s, v))
        for s, v in deps.items():
            if eng == "pe" and s is self.es["pe"]:
                continue
            self._wait(eng, s, v)

    def op(self, eng, fn, reads=(), writes=(), signal=True):
        self._deps(eng, reads, writes)
        ins = fn(self.e[eng])
        self.ninst[eng] += 1
        st = self.es[eng]
        if signal:
            st.cnt += 1
            ins.then_inc(st.sem, 1)
            v = st.cnt
        else:
            v = st.cnt + 1
        for r in reads:
            if r.rd.get(st, 0) < v:
                r.rd[st] = v
        for w in writes:
            w.lw = (st, v)
            w.rd = {}
        return ins

    def dma(self, q, ds, out, in_, reads=(), writes=(), nowait=False, **kw):
        if not nowait:
            self._wait(q, ds, ds.cnt)
        self._deps(q, reads, writes)
        ins = self.e[q].dma_start(out=out, in_=in_, **kw)
        self.ninst[q] += 1
        ds.cnt += 16
        ins.then_inc(ds.sem, 16)
        v = ds.cnt
        for r in reads:
            if r.rd.get(ds, 0) < v:
                r.rd[ds] = v
        for w in writes:
            w.lw = (ds, v)
            w.rd = {}
        return ins

    def barrier(self):
        for e in self.ENG:
            for f in self.ENG:
                if f != e:
                    self._wait(e, self.es[f], self.es[f].cnt)
            for ds in self.dstreams:
                if ds not in self.nobar:
                    self._wait(e, ds, ds.cnt)

    def finish(self):
        for ds in self.dstreams:
            self._wait("sp", ds, ds.cnt)
        for f in self.ENG:
            if f != "sp":
                self._wait("sp", self.es[f], self.es[f].cnt)


class SB:
    def __init__(self, nc, base, limit):
        self.nc, self.base, self.limit = nc, base, limit
        self.off = base
        self.n = 0

    def reset(self):
        self.off = self.base

    def t(self, shape, dtype):
        esz = 2 if dtype == BF16 else 4
        nbytes = int(np.prod(shape[1:])) * esz
        off = (self.off + 63) // 64 * 64
        assert off + nbytes <= self.limit, f"SBUF overflow: need {off + nbytes - self.limit} more bytes"
        self.off = off + nbytes
        self.n += 1
        return self.nc.alloc_sbuf_tensor_at(f"sb{self.n}", list(shape), dtype, offset=off)


class Ctx:
    pass


def emit_stats(P, C, hbt, r_hb, sq, r_sq, r_st, part=None):
    if len(sq) >= KC:
        if part in (None, "sq"):
            for k in range(KC):
                P.op("act", lambda e: e.activation(out=sq[k], in_=hbt[:, k, :], func=AF.Square),
                     reads=[r_hb[k]], writes=[r_sq[k]])
        if part in (None, "mm"):
            for k in range(KC):
                P.op("pe", lambda e: e.matmul(C.psS[:, :], lhsT=C.ones_b[:, :], rhs=sq[k], start=(k == 0), stop=(k == KC - 1)),
                     reads=[r_sq[k]], writes=[r_st], signal=(k == KC - 1))
        return
    for k in range(KC):
        s = k % len(sq)
        P.op("act", lambda e: e.activation(out=sq[s], in_=hbt[:, k, :], func=AF.Square),
             reads=[r_hb[k]], writes=[r_sq[s]])
        P.op("pe", lambda e: e.matmul(C.psS[:, :], lhsT=C.ones_b[:, :], rhs=sq[s], start=(k == 0), stop=(k == KC - 1)),
             reads=[r_sq[s]], writes=[r_st])


def emit_norm(P, C, hbt, r_hb, gcol, rstd, r_rstd, r_st, hn, r_hn, out_f32=None):
    P.op("act", lambda e: e.activation(out=rstd[:, :], in_=C.psS[:, :], func=AF.Sqrt, scale=1.0 / D, bias=C.eps_col[:, 0:1]),
         reads=[r_st], writes=[r_rstd])
    P.op("dve", lambda e: e.reciprocal(out=rstd[:, :], in_=rstd[:, :]), reads=[r_rstd], writes=[r_rstd])
    for k in range(KC):
        P.op("dve", lambda e: e.scalar_tensor_tensor(out=hn[:, k, :], in0=hbt[:, k, :], scalar=gcol[:, k:k + 1],
                                                     in1=rstd[:, :], op0=ALU.mult, op1=ALU.mult),
             reads=[r_hb[k], r_rstd], writes=[r_hn[k]])


def load_weight(P, C, dst, src_view, nchunk, res_list, per=1):
    rd = [C.cast_res[src_view.tensor.name]]
    for c0 in range(0, nchunk, per):
        c1 = min(nchunk, c0 + per)
        ds = C.dw[(C.dwi) % len(C.dw)]
        C.dwi += 1
        P.dma("sp", ds, dst[:, c0:c1, :], src_view[:, c0:c1, :], writes=res_list[c0:c1], reads=rd)


def ffn_phase(P, C, w1b, w3b, w2b, gcol, hsrc=None):
    sb = C.sb
    sb.reset()
    w1s = sb.t([128, KC, FF], BF16)
    w3s = sb.t([128, KC, FF], BF16)
    w2s = sb.t([128, FC, D], BF16)
    hb = [sb.t([128, KC, TT], F32) for _ in range(2)]
    hn = sb.t([128, KC, TT], BF16)
    act = sb.t([128, FC, TT], BF16)
    tmp = [sb.t([128, TT], F32) for _ in range(2)]
    rstd = sb.t([128, TT], F32)
    hv = C.hs.rearrange("(k p) s -> p k s", p=128)
    hvs = hv if hsrc is None else hsrc.rearrange("(k p) s -> p k s", p=128)
    r_hb = [P.res(KC) for _ in range(2)]
    P.dma("sp", C.dh[0], hb[0][:, :, :], hvs[:, :, 0:TT], writes=r_hb[0])
    FB = [(0, 768), (768, 1536), (1536, 2176), (2176, 2816)]
    fblk = [0] * 6 + [1] * 6 + [2] * 5 + [3] * 5
    r_w1, r_w3, r_w2 = P.res(4), P.res(4), P.res(4)
    w1v = w1b.rearrange("(k p) f -> p k f", p=128)
    w3v = w3b.rearrange("(k p) f -> p k f", p=128)
    w2v = w2b.rearrange("(c p) d -> p c d", p=128)
    rd1, rd3, rd2 = [C.cast_res[w1b.tensor.name]], [C.cast_res[w3b.tensor.name]], [C.cast_res[w2b.tensor.name]]
    for bi, (f0, f1) in enumerate(FB):
        P.dma("sp", C.dw[bi], w1s[:, :, f0:f1], w1v[:, :, f0:f1], writes=[r_w1[bi]], reads=rd1)
        P.dma("sp", C.dw[4 + bi], w3s[:, :, f0:f1], w3v[:, :, f0:f1], writes=[r_w3[bi]], reads=rd3)
    for bi in range(4):
        P.dma("sp", C.dw[8 + bi], w2s[:, :, bi * 256:(bi + 1) * 256], w2v[:, :, bi * 256:(bi + 1) * 256], writes=[r_w2[bi]], reads=rd2)
    r_hn = P.res(KC)
    r_st, r_rstd = P.res(), P.res()
    r_psG, r_psU, r_psO, r_tmp = P.res(2), P.res(2), P.res(2), P.res(2)
    r_act = P.res(FC)
    psG, psU, psO = C.ps[0:2], C.ps[2:4], C.ps[4:6]

    def load(i):
        b = i % 2
        P.dma("sp", C.dh[b], hb[b][:, :, :], hvs[:, :, i * TT:(i + 1) * TT], writes=r_hb[b])

    def gu(i):
        for f in range(FC):
            s = f % 2
            for k in range(KC):
                P.op("pe", lambda e: e.matmul(psG[s][:, :], lhsT=w1s[:, k, f * 128:(f + 1) * 128], rhs=hn[:, k, :],
                                              start=(k == 0), stop=(k == KC - 1)),
                     reads=[r_w1[fblk[f]], r_hn[k]], writes=[r_psG[s]], signal=(k == KC - 1))
            for k in range(KC):
                P.op("pe", lambda e: e.matmul(psU[s][:, :], lhsT=w3s[:, k, f * 128:(f + 1) * 128], rhs=hn[:, k, :],
                                              start=(k == 0), stop=(k == KC - 1)),
                     reads=[r_w3[fblk[f]], r_hn[k]], writes=[r_psU[s]], signal=(k == KC - 1))
            P.op("act", lambda e: e.activation(out=tmp[s][:, :], in_=psG[s][:, :], func=AF.Silu),
                 reads=[r_psG[s]], writes=[r_tmp[s]])
            P.op("dve", lambda e: e.tensor_tensor(out=act[:, f, :], in0=tmp[s][:, :], in1=psU[s][:, :], op=ALU.mult),
                 reads=[r_tmp[s], r_psU[s]], writes=[r_act[f]])

    sq = [hn[:, k, :] for k in range(KC)]
    r_sq = r_hn

    def down(i):
        b = i % 2
        for dc in range(KC):
            if dc == 2 and i + 1 < NT:
                b1 = (i + 1) % 2
                emit_stats(P, C, hb[b1], r_hb[b1], sq, r_sq, r_st, part="mm")
                emit_norm(P, C, hb[b1], r_hb[b1], gcol, rstd, r_rstd, r_st, hn, r_hn)
            s = dc % 2
            for f in range(FC):
                P.op("pe", lambda e: e.matmul(psO[s][:, :], lhsT=w2s[:, f, dc * 128:(dc + 1) * 128], rhs=act[:, f, :],
                                              start=(f == 0), stop=(f == FC - 1)),
                     reads=[r_w2[dc // 2], r_act[f]], writes=[r_psO[s]], signal=(f == FC - 1))
            P.op("dve", lambda e: e.scalar_tensor_tensor(out=hb[b][:, dc, :], in0=psO[s][:, :], scalar=0.5,
                                                         in1=hb[b][:, dc, :], op0=ALU.mult, op1=ALU.add),
                 reads=[r_psO[s], r_hb[b][dc]], writes=[r_hb[b][dc]])

    def store(i):
        b = i % 2
        P.dma("sp", C.dst[b], hv[:, :, i * TT:(i + 1) * TT], hb[b][:, :, :], reads=r_hb[b])

    emit_stats(P, C, hb[0], r_hb[0], sq, r_sq, r_st)
    emit_norm(P, C, hb[0], r_hb[0], gcol, rstd, r_rstd, r_st, hn, r_hn)
    for i in range(NT):
        if i + 1 < NT:
            load(i + 1)
        gu(i)
        if i + 1 < NT:
            b1 = (i + 1) % 2
            emit_stats(P, C, hb[b1], r_hb[b1], sq, r_sq, r_st, part="sq")
        down(i)
        store(i)
    P.barrier()


def final_phase(P, C, gcol):
    sb = C.sb
    sb.reset()
    hb = [sb.t([128, KC, TT], F32) for _ in range(2)]
    ob = [sb.t([128, KC, TT], F32) for _ in range(2)]
    sq = [sb.t([128, TT], BF16)[:, :] for _ in range(8)]
    rstd = sb.t([128, TT], F32)
    hv = C.hs.rearrange("(k p) s -> p k s", p=128)
    ov = C.outT.rearrange("(k p) s -> p k s", p=128)
    r_hb = [P.res(KC) for _ in range(2)]
    r_ob = [P.res(KC) for _ in range(2)]
    r_sq = P.res(8)
    r_st, r_rstd = P.res(), P.res()
    P.dma("sp", C.dh[0], hb[0][:, :, :], hv[:, :, 0:TT], writes=r_hb[0])
    for i in range(NT):
        b = i % 2
        if i + 1 < NT:
            P.dma("sp", C.dh[1 - b], hb[1 - b][:, :, :], hv[:, :, (i + 1) * TT:(i + 2) * TT], writes=r_hb[1 - b])
        emit_stats(P, C, hb[b], r_hb[b], sq, r_sq, r_st)
        emit_norm(P, C, hb[b], r_hb[b], gcol, rstd, r_rstd, r_st, ob[b], r_ob[b])
        P.dma("sp", C.dst[b], ov[:, :, i * TT:(i + 1) * TT], ob[b][:, :, :], reads=r_ob[b])
    P.barrier()


def conv_phase(P, C, W, gcol):
    sb = C.sb
    hv = C.hs.rearrange("(k p) s -> p k s", p=128)
    GLv = C.GL.rearrange("(k p) s -> p k s", p=128)
    sb.reset()
    pw1s = sb.t([128, KC, 2 * D], BF16)
    hb = [sb.t([128, KC, TT], F32) for _ in range(2)]
    hn_l = [sb.t([128, KC, TT], BF16) for _ in range(2)]
    sq = [sb.t([128, TT], BF16)[:, :] for _ in range(8)]
    rstd = sb.t([128, TT], F32)
    sig = [sb.t([128, TT], F32) for _ in range(2)]
    glb = [sb.t([128, TT], BF16) for _ in range(2)]
    zpad = sb.t([128, KC, 16], BF16)
    r_hb = [P.res(KC) for _ in range(2)]
    P.dma("sp", C.dh[0], hb[0][:, :, :], hv[:, :, 0:TT], writes=r_hb[0])
    r_pw1 = P.res(KC)
    load_weight(P, C, pw1s, W["pw1b"].rearrange("(k p) f -> p k f", p=128), KC, r_pw1, per=2)
    r_z = P.res()
    P.op("dve", lambda e: e.memset(zpad[:, :, :], 0.0), writes=[r_z])
    P.dma("sp", C.dm[0], GLv[:, :, 0:16], zpad[:, :, :], reads=[r_z])
    P.dma("sp", C.dm[1], GLv[:, :, 16 + S:32 + S], zpad[:, :, :], reads=[r_z])
    r_hn_l = [P.res(KC) for _ in range(2)]
    r_sq = P.res(8)
    r_st, r_rstd = P.res(), P.res()
    r_psA, r_psG, r_sig, r_glb = P.res(2), P.res(2), P.res(2), P.res(2)
    psA, psG = C.ps[0:2], C.ps[2:4]
    b1 = C.cols["conv_b_pw1"]
    def front(i):
        emit_stats(P, C, hb[i % 2], r_hb[i % 2], sq, r_sq, r_st)
        emit_norm(P, C, hb[i % 2], r_hb[i % 2], gcol, rstd, r_rstd, r_st, hn_l[i % 2], r_hn_l[i % 2])

    front(0)
    for i in range(NT):
        b = i % 2
        hn, r_hn = hn_l[b], r_hn_l[b]
        if i + 1 < NT:
            P.dma("sp", C.dh[1 - b], hb[1 - b][:, :, :], hv[:, :, (i + 1) * TT:(i + 2) * TT], writes=r_hb[1 - b])
        for c in range(KC):
            if c == 3 and i + 1 < NT:
                front(i + 1)
            s = c % 2
            for k in range(KC):
                P.op("pe", lambda e: e.matmul(psA[s][:, :], lhsT=pw1s[:, k, c * 128:(c + 1) * 128], rhs=hn[:, k, :],
                                              start=(k == 0), stop=(k == KC - 1)),
                     reads=[r_pw1[k], r_hn[k]], writes=[r_psA[s]], signal=(k == KC - 1))
            for k in range(KC):
                P.op("pe", lambda e: e.matmul(psG[s][:, :], lhsT=pw1s[:, k, D + c * 128:D + (c + 1) * 128], rhs=hn[:, k, :],
                                              start=(k == 0), stop=(k == KC - 1)),
                     reads=[r_pw1[k], r_hn[k]], writes=[r_psG[s]], signal=(k == KC - 1))
            P.op("act", lambda e: e.activation(out=sig[s][:, :], in_=psG[s][:, :], func=AF.Sigmoid, bias=b1[:, 8 + c:9 + c]),
                 reads=[r_psG[s]], writes=[r_sig[s]])
            P.op("dve", lambda e: e.scalar_tensor_tensor(out=glb[s][:, :], in0=psA[s][:, :], scalar=b1[:, c:c + 1],
                                                         in1=sig[s][:, :], op0=ALU.add, op1=ALU.mult),
                 reads=[r_psA[s], r_sig[s]], writes=[r_glb[s]])
            P.dma("sp", C.dst[s], C.GL[c * 128:(c + 1) * 128, 16 + i * TT:16 + (i + 1) * TT], glb[s][:, :], reads=[r_glb[s]])
    P.barrier()
    sb.reset()
    pw2s = sb.t([128, KC, D], BF16)
    dg = sb.t([128, KC, 31, 128], BF16)
    idb = sb.t([128, 128], BF16)
    idf = sb.t([128, 128], F32)
    hb = [sb.t([128, KC, TT], F32) for _ in range(2)]
    xb = [sb.t([128, KC, TT + 32], BF16) for _ in range(2)]
    zb = sb.t([128, KC, TT], F32)
    sqz = [sb.t([128, TT], F32) for _ in range(2)]
    yb = sb.t([128, KC, TT], BF16)
    mu = sb.t([128, TT], F32)
    var = sb.t([128, TT], F32)
    tmp = [sb.t([128, TT], F32) for _ in range(2)]
    r_pw2 = P.res(KC)
    load_weight(P, C, pw2s, W["pw2b"].rearrange("(k p) f -> p k f", p=128), KC, r_pw2, per=4)
    wdw = C.cols["conv_w_dw"]
    bdw, lng, lnb, b2 = C.cols["conv_b_dw"], C.cols["conv_ln_g"], C.cols["conv_ln_b"], C.cols["conv_b_pw2"]
    r_id, r_dg = P.res(), P.res()
    P.dma("sp", C.dm[2], idb[:, :], C.tabs["ID"], writes=[r_id])
    P.op("dve", lambda e: e.tensor_copy(out=idf[:, :], in_=idb[:, :]), reads=[r_id], writes=[r_id])
    for c in range(KC):
        for j in range(31):
            P.op("dve", lambda e: e.tensor_scalar(out=dg[:, c, j, :], in0=idf[:, :], scalar1=wdw[:, c * 31 + j:c * 31 + j + 1],
                                                  scalar2=None, op0=ALU.mult), reads=[r_id], writes=[r_dg])
    r_hb = [P.res(KC) for _ in range(2)]
    r_xb, r_sqz, r_tmp = P.res(2), P.res(2), P.res(2)
    r_zb, r_yb = P.res(KC), P.res(KC)
    r_s1, r_s2, r_mu, r_var = P.res(), P.res(), P.res(), P.res()
    r_psO, r_psC = P.res(2), P.res(2)
    psS1, psS2, psO, psC = C.ps[0], C.ps[1], C.ps[2:4], C.ps[4:6]
    GLv3 = C.GL.rearrange("(k p) s -> p k s", p=128)

    def loadt(i):
        b = i % 2
        P.dma("sp", C.dh[b], hb[b][:, :, :], hv[:, :, i * TT:(i + 1) * TT], writes=r_hb[b])
        P.dma("sp", C.dm[b], xb[b][:, :, 0:TT + 30], GLv3[:, :, i * TT + 1:i * TT + 1 + TT + 30], writes=[r_xb[b]])

    loadt(0)
    for i in range(NT):
        b = i % 2
        if i + 1 < NT:
            loadt(i + 1)
        for c in range(KC):
            s = c % 2
            for j in range(31):
                P.op("pe", lambda e: e.matmul(psC[s][:, :], lhsT=dg[:, c, j, :], rhs=xb[b][:, c, j:j + TT], start=(j == 0), stop=(j == 30)),
                     reads=[r_dg, r_xb[b]], writes=[r_psC[s]], signal=(j == 30))
            P.op("dve", lambda e: e.tensor_scalar(out=zb[:, c, :], in0=psC[s][:, :], scalar1=bdw[:, c:c + 1], scalar2=None, op0=ALU.add),
                 reads=[r_psC[s]], writes=[r_zb[c]])
            P.op("act", lambda e: e.activation(out=sqz[s][:, :], in_=zb[:, c, :], func=AF.Square),
                 reads=[r_zb[c]], writes=[r_sqz[s]])
            for cp in ([c - 1] if c >= 1 else []) + ([c] if c == KC - 1 else []):
                sp_ = cp % 2
                P.op("pe", lambda e: e.matmul(psS1[:, :], lhsT=C.ones_f[:, :], rhs=zb[:, cp, :], start=(cp == 0), stop=(cp == KC - 1)),
                     reads=[r_zb[cp]], writes=[r_s1])
                P.op("pe", lambda e: e.matmul(psS2[:, :], lhsT=C.ones_f[:, :], rhs=sqz[sp_][:, :], start=(cp == 0), stop=(cp == KC - 1)),
                     reads=[r_sqz[sp_]], writes=[r_s2])
        P.op("act", lambda e: e.activation(out=mu[:, :], in_=psS1[:, :], func=AF.Copy, scale=1.0 / D), reads=[r_s1], writes=[r_mu])
        P.op("dve", lambda e: e.tensor_tensor(out=var[:, :], in0=mu[:, :], in1=mu[:, :], op=ALU.mult), reads=[r_mu], writes=[r_var])
        P.op("dve", lambda e: e.scalar_tensor_tensor(out=var[:, :], in0=psS2[:, :], scalar=1.0 / D, in1=var[:, :],
                                                     op0=ALU.mult, op1=ALU.subtract), reads=[r_s2, r_var], writes=[r_var])
        P.op("act", lambda e: e.activation(out=var[:, :], in_=var[:, :], func=AF.Sqrt, bias=C.eps_col[:, 0:1]), reads=[r_var], writes=[r_var])
        P.op("dve", lambda e: e.reciprocal(out=var[:, :], in_=var[:, :]), reads=[r_var], writes=[r_var])
        for c in range(KC):
            s = c % 2
            P.op("dve", lambda e: e.tensor_tensor(out=tmp[s][:, :], in0=zb[:, c, :], in1=mu[:, :], op=ALU.subtract),
                 reads=[r_zb[c], r_mu], writes=[r_tmp[s]])
            P.op("dve", lambda e: e.tensor_tensor(out=tmp[s][:, :], in0=tmp[s][:, :], in1=var[:, :], op=ALU.mult),
                 reads=[r_tmp[s], r_var], writes=[r_tmp[s]])
            P.op("act", lambda e: e.activation(out=yb[:, c, :], in_=tmp[s][:, :], func=AF.Silu, scale=lng[:, c:c + 1], bias=lnb[:, c:c + 1]),
                 reads=[r_tmp[s]], writes=[r_yb[c]])
        for dc in range(KC):
            s = dc % 2
            for c in range(KC):
                P.op("pe", lambda e: e.matmul(psO[s][:, :], lhsT=pw2s[:, c, dc * 128:(dc + 1) * 128], rhs=yb[:, c, :],
                                              start=(c == 0), stop=(c == KC - 1)),
                     reads=[r_pw2[c], r_yb[c]], writes=[r_psO[s]], signal=(c == KC - 1))
            P.op("dve", lambda e: e.scalar_tensor_tensor(out=hb[b][:, dc, :], in0=psO[s][:, :], scalar=b2[:, dc:dc + 1],
                                                         in1=hb[b][:, dc, :], op0=ALU.add, op1=ALU.add),
                 reads=[r_psO[s], r_hb[b][dc]], writes=[r_hb[b][dc]])
        P.dma("sp", C.dst[b], hv[:, :, i * TT:(i + 1) * TT], hb[b][:, :, :], reads=r_hb[b])
    P.barrier()


def fourier_phase(P, C, W, gcol):
    sb = C.sb
    hv = C.hs.rearrange("(k p) s -> p k s", p=128)
    sb.reset()
    hb = [sb.t([128, KC, TT], F32) for _ in range(2)]
    hn = [sb.t([128, KC, TT], BF16) for _ in range(2)]
    sq = [sb.t([128, TT], BF16)[:, :] for _ in range(8)]
    rstd = sb.t([128, TT], F32)
    r_hb = [P.res(KC) for _ in range(2)]
    r_hn = [P.res(KC) for _ in range(2)]
    r_sq = P.res(8)
    r_st, r_rstd = P.res(), P.res()
    UTv = C.UT.rearrange("(k p) s -> p k s", p=128)
    P.dma("sp", C.dh[0], hb[0][:, :, :], hv[:, :, 0:TT], writes=r_hb[0])
    for i in range(NT):
        b = i % 2
        if i + 1 < NT:
            P.dma("sp", C.dh[1 - b], hb[1 - b][:, :, :], hv[:, :, (i + 1) * TT:(i + 2) * TT], writes=r_hb[1 - b])
        emit_stats(P, C, hb[b], r_hb[b], sq, r_sq, r_st)
        emit_norm(P, C, hb[b], r_hb[b], gcol, rstd, r_rstd, r_st, hn[b], r_hn[b])
        P.dma("sp", C.dst[b], UTv[:, :, i * TT:(i + 1) * TT], hn[b][:, :, :], reads=r_hn[b])
    P.barrier()
    sb.reset()
    F1 = sb.t([64, 192], BF16)
    Mt = sb.t([128, 64, 2, 128], BF16)
    uS = [sb.t([64, 128, 128], BF16) for _ in range(2)]
    Z = sb.t([128, 128, 192], BF16)
    PTs = sb.t([128, S], BF16)
    QTs = sb.t([128, S], BF16)
    r_tab = P.res()
    P.dma("pool", C.dm[0], F1[:, :], C.tabs["F1"], writes=[r_tab])
    P.dma("pool", C.dm[1], Mt[:, :, :, :], C.tabs["Mt"], writes=[r_tab])
    r_uS = P.res(2)
    r_Z = P.res(64)
    r_psZ = P.res(2)
    r_psP, r_psQ = P.res(2), P.res(2)
    r_PT, r_QT = P.res(), P.res()
    psZ, psP, psQ = C.ps[0:2], C.ps[2:4], C.ps[4:6]
    UTa = C.UT.rearrange("ch (a b) -> a ch b", b=128)
    PTv = PTs[:, :].rearrange("p (d c) -> p c d", c=64)
    QTv = QTs[:, :].rearrange("p (d c) -> p c d", c=64)

    def loadu(cc):
        for hlf in range(2):
            P.dma("sp", C.dh[hlf], uS[cc % 2][:, hlf * 64:(hlf + 1) * 64, :],
                  UTa[:, cc * 128 + hlf * 64:cc * 128 + (hlf + 1) * 64, :], writes=[r_uS[cc % 2]])

    loadu(0)
    for cc in range(KC):
        u = uS[cc % 2]
        if cc + 1 < KC:
            loadu(cc + 1)
        for pr in range(64):
            s = pr % 2
            for q in range(2):
                ch = 2 * pr + q
                P.op("pe", lambda e: e.matmul(psZ[s][:, q * 192:(q + 1) * 192], lhsT=u[:, ch, :], rhs=F1[:, :], start=True, stop=True),
                     reads=[r_uS[cc % 2], r_tab], writes=[r_psZ[s]], signal=(q == 1))
            eng = "act" if pr % 2 == 0 else "dve"
            if eng == "act":
                P.op("act", lambda e: e.copy(out=Z[:, 2 * pr:2 * pr + 2, :], in_=psZ[s][:, 0:384].rearrange("p (q f) -> p q f", q=2)),
                     reads=[r_psZ[s]], writes=[r_Z[pr]])
            else:
                P.op("dve", lambda e: e.tensor_copy(out=Z[:, 2 * pr:2 * pr + 2, :], in_=psZ[s][:, 0:384].rearrange("p (q f) -> p q f", q=2)),
                     reads=[r_psZ[s]], writes=[r_Z[pr]])
        for c4 in range(16):
            s = c4 % 2
            for q in range(4):
                c = 4 * c4 + q
                P.op("pe", lambda e: e.matmul(psP[s][:, q * 128:(q + 1) * 128], lhsT=Z[:, :, c], rhs=Mt[:, c, 0, :], start=True, stop=False),
                     reads=r_Z + [r_tab], writes=[r_psP[s]], signal=False)
                P.op("pe", lambda e: e.matmul(psP[s][:, q * 128:(q + 1) * 128], lhsT=Z[:, :, 128 + c], rhs=Mt[:, c, 1, :], start=False, stop=True),
                     reads=r_Z + [r_tab], writes=[r_psP[s]], signal=(q == 3))
            for q in range(4):
                c = 4 * c4 + q
                P.op("pe", lambda e: e.matmul(psQ[s][:, q * 128:(q + 1) * 128], lhsT=Z[:, :, c], rhs=Mt[:, c, 1, :], start=True, stop=False),
                     reads=r_Z + [r_tab], writes=[r_psQ[s]], signal=False)
                P.op("pe", lambda e: e.matmul(psQ[s][:, q * 128:(q + 1) * 128], lhsT=Z[:, :, 64 + c], rhs=Mt[:, c, 0, :], start=False, stop=True),
                     reads=r_Z + [r_tab], writes=[r_psQ[s]], signal=(q == 3))
            P.op("act", lambda e: e.copy(out=PTv[:, 4 * c4:4 * c4 + 4, :], in_=psP[s][:, :].rearrange("p (q d) -> p q d", q=4)),
                 reads=[r_psP[s]], writes=[r_PT])
            P.op("dve", lambda e: e.tensor_copy(out=QTv[:, 4 * c4:4 * c4 + 4, :], in_=psQ[s][:, :].rearrange("p (q d) -> p q d", q=4)),
                 reads=[r_psQ[s]], writes=[r_QT])
        P.dma("sp", C.dst[0], C.PT[cc * 128:(cc + 1) * 128, :], PTs[:, :], reads=[r_PT])
        P.dma("sp", C.dst[1], C.QT[cc * 128:(cc + 1) * 128, :], QTs[:, :], reads=[r_QT])
    P.barrier()
    sb.reset()
    wfs = sb.t([128, KC, D], BF16)
    CS = sb.t([128, 2, 2, 256], BF16)
    hb = [sb.t([128, KC, TT], F32) for _ in range(2)]
    pq = [[sb.t([128, KC, TT], BF16) for _ in range(2)] for _ in range(2)]
    mx = sb.t([128, KC, TT], BF16)
    r_wf = P.res(KC)
    load_weight(P, C, wfs, W["wfb"].rearrange("(k p) f -> p k f", p=128), KC, r_wf, per=4)
    r_cs = P.res()
    P.dma("pool", C.dm[0], CS[:, :, :, :], C.tabs["CS"], writes=[r_cs])
    r_hb = [P.res(KC) for _ in range(2)]
    r_p = [P.res() for _ in range(2)]
    r_q = [P.res() for _ in range(2)]
    r_mx = P.res(KC)
    r_psM, r_psO = P.res(2), P.res(2)
    psM, psO = C.ps[0:2], C.ps[2:4]
    PTd = C.PT.rearrange("(k p) s -> p k s", p=128)
    QTd = C.QT.rearrange("(k p) s -> p k s", p=128)
    bfc = C.cols["fnet_b"]
    scl = 1.0 / math.sqrt(S * 256.0)

    def loadt(i):
        b = i % 2
        P.dma("sp", C.dh[b], hb[b][:, :, :], hv[:, :, i * TT:(i + 1) * TT], writes=r_hb[b])
        P.dma("sp", C.dm[b], pq[b][0][:, :, :], PTd[:, :, i * TT:(i + 1) * TT], writes=[r_p[b]])
        P.dma("sp", C.dm[2 + b], pq[b][1][:, :, :], QTd[:, :, i * TT:(i + 1) * TT], writes=[r_q[b]])

    loadt(0)
    for i in range(NT):
        b = i % 2
        if i + 1 < NT:
            loadt(i + 1)
        for g in range(4):
            for j in range(2):
                oc = 2 * g + j
                s = oc % 2
                for n in range(2):
                    P.op("pe", lambda e: e.matmul(psM[s][:, :], lhsT=CS[:, n, 0, j * 128:(j + 1) * 128], rhs=pq[b][0][:, 2 * g + n, :],
                                                  start=(n == 0), stop=False),
                         reads=[r_cs, r_p[b]], writes=[r_psM[s]], signal=False)
                    P.op("pe", lambda e: e.matmul(psM[s][:, :], lhsT=CS[:, n, 1, j * 128:(j + 1) * 128], rhs=pq[b][1][:, 2 * g + n, :],
                                                  start=False, stop=(n == 1)),
                         reads=[r_cs, r_q[b]], writes=[r_psM[s]], signal=(n == 1))
                P.op("act", lambda e: e.activation(out=mx[:, oc, :], in_=psM[s][:, :], func=AF.Copy, scale=scl),
                     reads=[r_psM[s]], writes=[r_mx[oc]])
        for dc in range(KC):
            s = dc % 2
            for k in range(KC):
                P.op("pe", lambda e: e.matmul(psO[s][:, :], lhsT=wfs[:, k, dc * 128:(dc + 1) * 128], rhs=mx[:, k, :],
                                              start=(k == 0), stop=(k == KC - 1)),
                     reads=[r_wf[k], r_mx[k]], writes=[r_psO[s]], signal=(k == KC - 1))
            P.op("dve", lambda e: e.scalar_tensor_tensor(out=hb[b][:, dc, :], in0=psO[s][:, :], scalar=bfc[:, dc:dc + 1],
                                                         in1=hb[b][:, dc, :], op0=ALU.add, op1=ALU.add),
                 reads=[r_psO[s], r_hb[b][dc]], writes=[r_hb[b][dc]])
        P.dma("sp", C.dst[b], hv[:, :, i * TT:(i + 1) * TT], hb[b][:, :, :], reads=r_hb[b])
    P.barrier()


PADK = 1024
LS = 2048


def attn_phase(P, C, W, gcol):
    sb = C.sb
    hv = C.hs.rearrange("(k p) s -> p k s", p=128)
    sb.reset()
    wq = sb.t([128, KC, 4608], BF16)
    hb = [sb.t([128, KC, TT], F32) for _ in range(2)]
    hn_l = [sb.t([128, KC, TT], BF16) for _ in range(2)]
    sq = [sb.t([128, TT], BF16)[:, :] for _ in range(8)]
    rstd = sb.t([128, TT], F32)
    qk = [sb.t([128, TT], BF16) for _ in range(4)]
    va = [sb.t([128, 8, 65], BF16) for _ in range(4)]
    zt = sb.t([128, 8 * 520], BF16)
    r_hb = [P.res(KC) for _ in range(2)]
    P.dma("sp", C.dh[0], hb[0][:, :, :], hv[:, :, 0:TT], writes=r_hb[0])
    r_wq = P.res(KC)
    load_weight(P, C, wq, W["wqkvb"].rearrange("(k p) f -> p k f", p=128), KC, r_wq, per=1)
    r_z = P.res()
    r_va = P.res(4)
    P.op("pool", lambda e: e.memset(zt[:, :], 0.0), writes=[r_z])
    for q in range(4):
        P.op("pool", lambda e: e.memset(va[q][:, :, :], 1.0), writes=[r_va[q]])
    for g in range(3):
        for side in range(2):
            c0 = 0 if side == 0 else PADK + S
            P.dma("sp", C.dm[side], C.KT[g].rearrange("(c p) s -> p c s", p=128)[:, :, c0:c0 + PADK],
                  zt[:, 0:4 * PADK].rearrange("p (c s) -> p c s", c=4), reads=[r_z])
            P.dma("sp", C.dm[2 + side], C.VA[g][c0:c0 + PADK, :].rearrange("(p k) f -> p (k f)", p=128), zt[:, :], reads=[r_z])
    r_hn_l = [P.res(KC) for _ in range(2)]
    r_sq = P.res(8)
    r_st, r_rstd = P.res(), P.res()
    r_ps = P.res(4)
    r_qk = P.res(4)
    ps = C.ps[0:4]
    dq = C.dst + C.dm[0:2]
    n = 0

    def front(i):
        emit_stats(P, C, hb[i % 2], r_hb[i % 2], sq, r_sq, r_st)
        emit_norm(P, C, hb[i % 2], r_hb[i % 2], gcol, rstd, r_rstd, r_st, hn_l[i % 2], r_hn_l[i % 2])

    front(0)
    for i in range(NT):
        b = i % 2
        hn, r_hn = hn_l[b], r_hn_l[b]
        if i + 1 < NT:
            P.dma("sp", C.dh[1 - b], hb[1 - b][:, :, :], hv[:, :, (i + 1) * TT:(i + 2) * TT], writes=r_hb[1 - b])
        for g in range(3):
            if g == 1 and i + 1 < NT:
                front(i + 1)
            for t in range(2):
                for c in range(4):
                    s = n % 4
                    col = g * 1536 + t * 512 + c * 128
                    for k in range(KC):
                        P.op("pe", lambda e: e.matmul(ps[s][:, :], lhsT=wq[:, k, col:col + 128], rhs=hn[:, k, :],
                                                      start=(k == 0), stop=(k == KC - 1)),
                             reads=[r_wq[k], r_hn[k]], writes=[r_ps[s]], signal=(k == KC - 1))
                    if n % 2 == 0:
                        P.op("act", lambda e: e.copy(out=qk[s][:, :], in_=ps[s][:, :]), reads=[r_ps[s]], writes=[r_qk[s]])
                    else:
                        P.op("dve", lambda e: e.tensor_copy(out=qk[s][:, :], in_=ps[s][:, :]), reads=[r_ps[s]], writes=[r_qk[s]])
                    if t == 0:
                        dst = C.AQ[g][c * 128:(c + 1) * 128, i * TT:(i + 1) * TT]
                    else:
                        dst = C.KT[g][c * 128:(c + 1) * 128, PADK + i * TT:PADK + (i + 1) * TT]
                    P.dma("sp", dq[s], dst, qk[s][:, :], reads=[r_qk[s]])
                    n += 1
            for j in range(4):
                s = n % 4
                col = g * 1536 + 1024
                for k in range(KC):
                    P.op("pe", lambda e: e.matmul(ps[s][:, :], lhsT=hn[:, k, j * 128:(j + 1) * 128], rhs=wq[:, k, col:col + 512],
                                                  start=(k == 0), stop=(k == KC - 1)),
                         reads=[r_wq[k], r_hn[k]], writes=[r_ps[s]], signal=(k == KC - 1))
                src = ps[s][:, :].rearrange("p (h e) -> p h e", h=8)
                if n % 2 == 0:
                    P.op("act", lambda e: e.copy(out=va[s][:, :, 0:64], in_=src), reads=[r_ps[s]], writes=[r_va[s]])
                else:
                    P.op("dve", lambda e: e.tensor_copy(out=va[s][:, :, 0:64], in_=src), reads=[r_ps[s]], writes=[r_va[s]])
                r0 = PADK + i * TT + j * 128
                P.dma("sp", dq[s], C.VA[g][r0:r0 + 128, :], va[s][:, :, :].rearrange("p h e -> p (h e)"), reads=[r_va[s]])
                n += 1
    P.barrier()
    if os.environ.get("ATT_STOP") == "1":
        return
    sb.reset()
    ET = sb.t([128, 3, 2, 8, 128], F32)
    Qs = [sb.t([64, 4, LS], BF16) for _ in range(2)]
    Ks = [sb.t([64, 4, LS + 2048], BF16) for _ in range(2)]
    Vs = [sb.t([128, 32 * 260 + 260], BF16) for _ in range(2)]
    eb = [[sb.t([128, 512], F32) for _ in range(2)] for _ in range(2)]
    pT = [[sb.t([128, 4, 128], BF16) for _ in range(2)] for _ in range(2)]
    nz = [sb.t([128, 260], F32) for _ in range(2)]
    r_et = P.res()
    P.dma("sp", C.dm[0], ET[:, :, :, :, :].rearrange("p a b c d -> p (a b c d)"), C.tabs["EA"], writes=[r_et])
    r_Q, r_K, r_V = P.res(2), P.res(2), P.res(2)
    r_psS, r_eb, r_pT = [P.res(2) for _ in range(2)], [P.res(2) for _ in range(2)], [P.res(2) for _ in range(2)]
    r_psO, r_nz = P.res(2), P.res(2)
    psS, psO = [C.ps[0:2], C.ps[2:4]], C.ps[4:6]
    sets = [(g, sp, hq) for g in range(3) for sp in range(S // LS) for hq in range(2)]
    if os.environ.get("ATT_SETS"):
        lo_, hi_ = os.environ["ATT_SETS"].split(":")
        sets = sets[int(lo_):int(hi_)]

    def loadset(idx):
        g, sp, hq = sets[idx]
        dil = DIL[g]
        b = idx % 2
        t0 = sp * LS
        NB = LS // (128 * dil)
        QTv = C.AQ[g].rearrange("(h e) s -> e h s", e=64)
        KTv = C.KT[g].rearrange("(h e) s -> e h s", e=64)
        P.dma("sp", C.dh[b], Qs[b][:, :, :], QTv[:, 4 * hq:4 * hq + 4, t0:t0 + LS], writes=[r_Q[b]])
        kw = LS + 128 * dil
        k0 = PADK + t0 - 64 * dil
        P.dma("sp", C.dm[b], Ks[b][:, :, 0:kw], KTv[:, 4 * hq:4 * hq + 4, k0:k0 + kw], writes=[r_K[b]])
        vsrc = C.VA[g][k0:k0 + (NB + 1) * 128 * dil, :].rearrange("(c k d) f -> k c d f", k=128, d=dil)[:, :, :, hq * 260:(hq + 1) * 260]
        vdst = Vs[b][:, 0:(NB + 1) * dil * 260].rearrange("p (c d f) -> p c d f", c=NB + 1, d=dil)
        if dil == 1:
            P.dma("sp", C.dm[2 + b], vdst[:, :, 0, :], vsrc[:, :, 0, :], writes=[r_V[b]])
        else:
            for cch in range(NB + 1):
                P.dma("sp", C.dm[2 + b], vdst[:, cch, :, :], vsrc[:, cch, :, :], writes=[r_V[b]], nowait=(cch > 0))

    loadset(0)
    un = 0
    for idx in range(len(sets)):
        g, sp, hq = sets[idx]
        dil = DIL[g]
        b = idx % 2
        t0 = sp * LS
        NB = LS // (128 * dil)
        if idx + 1 < len(sets):
            loadset(idx + 1)
        Vv = Vs[b][:, 0:(NB + 1) * dil * 260].rearrange("p (c d f) -> p c d f", c=NB + 1, d=dil)

        def front(nb, r, u2):
            for AB in range(2):
                koff = r + dil * (128 * nb + AB * 128)
                qoff = r + dil * 128 * nb
                for hh in range(4):
                    P.op("pe", lambda e: e.matmul(psS[u2][AB][:, hh * 128:(hh + 1) * 128],
                                                  lhsT=Ks[b][:, hh, koff:koff + 127 * dil + 1:dil],
                                                  rhs=Qs[b][:, hh, qoff:qoff + 127 * dil + 1:dil], start=True, stop=True),
                         reads=[r_K[b], r_Q[b]], writes=[r_psS[u2][AB]], signal=(hh == 3))
                P.op("act", lambda e: e.activation(out=eb[u2][AB][:, :], in_=psS[u2][AB][:, :], func=AF.Exp, scale=0.125),
                     reads=[r_psS[u2][AB]], writes=[r_eb[u2][AB]])
                P.op("dve", lambda e: e.tensor_tensor(out=pT[u2][AB][:, :, :].rearrange("p h q -> p (h q)"), in0=eb[u2][AB][:, :],
                                                      in1=ET[:, g, AB, 4 * hq:4 * hq + 4, :].rearrange("p h q -> p (h q)"), op=ALU.mult),
                     reads=[r_eb[u2][AB], r_et], writes=[r_pT[u2][AB]])

        def back(nb, r, u2, uidx):
            for hh in range(4):
                for AB in range(2):
                    P.op("pe", lambda e: e.matmul(psO[u2][:, hh * 65:(hh + 1) * 65], lhsT=pT[u2][AB][:, hh, :],
                                                  rhs=Vv[:, nb + AB, r, hh * 65:(hh + 1) * 65], start=(AB == 0), stop=(AB == 1)),
                         reads=[r_pT[u2][AB], r_V[b]], writes=[r_psO[u2]], signal=(hh == 3 and AB == 1))
            if uidx % 2 == 0:
                P.op("act", lambda e: e.copy(out=nz[u2][:, :], in_=psO[u2][:, 0:260]), reads=[r_psO[u2]], writes=[r_nz[u2]])
            else:
                P.op("dve", lambda e: e.tensor_copy(out=nz[u2][:, :], in_=psO[u2][:, 0:260]), reads=[r_psO[u2]], writes=[r_nz[u2]])
            tq0 = t0 + dil * 128 * nb
            dstv = C.NZ[g][tq0:tq0 + 128 * dil, :].rearrange("(q d) f -> q d f", d=dil)[:, r, hq * 260:(hq + 1) * 260]
            P.dma("sp", C.dst[u2], dstv, nz[u2][:, :], reads=[r_nz[u2]])

        pend = None
        for nb in range(NB):
            for r in range(dil):
                u2 = un % 2
                front(nb, r, u2)
                if pend is not None:
                    back(*pend)
                pend = (nb, r, u2, un)
                un += 1
        back(*pend)
    P.barrier()
    if os.environ.get("ATT_STOP") == "2":
        return
    sb.reset()
    wos = sb.t([128, 4, D], BF16)
    ident = sb.t([128, 128], BF16)
    hb = [sb.t([128, KC, TT], F32) for _ in range(2)]
    nzl = [[sb.t([128, 4, 520], F32) for _ in range(3)] for _ in range(2)]
    nsum = [sb.t([128, 8, 65], F32) for _ in range(2)]
    rz = [sb.t([128, 8, 1], F32) for _ in range(2)]
    ob = [sb.t([128, 8, 64], BF16) for _ in range(2)]
    oT = sb.t([128, 4, TT], BF16)
    r_wo = P.res(4)
    load_weight(P, C, wos, W["wob"].rearrange("(k p) f -> p k f", p=128), 4, r_wo, per=4)
    r_id = P.res()
    P.dma("sp", C.dm[0], ident[:, :], C.tabs["ID"], writes=[r_id])
    r_hb = [P.res(KC) for _ in range(2)]
    r_nzl = [P.res(3) for _ in range(2)]
    r_ns, r_rz, r_ob = P.res(2), P.res(2), P.res(2)
    r_psT = P.res(2)
    r_oT = P.res(4)
    r_psO = P.res(2)
    psT, psO = C.psb, C.ps[0:2]
    dl = [C.dm[0:2], C.dm[2:4], C.dw[0:2]]

    def loadt(i):
        b = i % 2
        P.dma("sp", C.dh[b], hb[b][:, :, :], hv[:, :, i * TT:(i + 1) * TT], writes=r_hb[b])
        for g in range(3):
            P.dma("sp", dl[g][b], nzl[b][g][:, :, :], C.NZ[g][i * TT:(i + 1) * TT, :].rearrange("(j p) f -> p j f", p=128),
                  writes=[r_nzl[b][g]])

    loadt(0)
    for i in range(NT):
        b = i % 2
        if i + 1 < NT:
            loadt(i + 1)
        for j in range(4):
            s = j % 2
            nsf = nsum[s][:, :, :].rearrange("p h e -> p (h e)")
            P.op("pool", lambda e: e.tensor_tensor(out=nsf, in0=nzl[b][0][:, j, :], in1=nzl[b][1][:, j, :], op=ALU.add),
                 reads=[r_nzl[b][0], r_nzl[b][1]], writes=[r_ns[s]])
            P.op("pool", lambda e: e.tensor_tensor(out=nsf, in0=nsf, in1=nzl[b][2][:, j, :], op=ALU.add),
                 reads=[r_nzl[b][2], r_ns[s]], writes=[r_ns[s]])
            P.op("dve", lambda e: e.reciprocal(out=rz[s][:, :, :], in_=nsum[s][:, :, 64:65]), reads=[r_ns[s]], writes=[r_rz[s]])
            P.op("dve", lambda e: e.tensor_tensor(out=ob[s][:, :, :], in0=nsum[s][:, :, 0:64], in1=rz[s][:, :, :].to_broadcast([128, 8, 64]),
                                                  op=ALU.mult), reads=[r_ns[s], r_rz[s]], writes=[r_ob[s]])
            for fc in range(4):
                P.op("pe", lambda e: e.transpose(psT[s][:, fc * 128:(fc + 1) * 128],
                                                 ob[s][:, 2 * fc:2 * fc + 2, :].rearrange("p h e -> p (h e)"), ident[:, :]),
                     reads=[r_ob[s], r_id], writes=[r_psT[s]], signal=(fc == 3))
            P.op("act", lambda e: e.copy(out=oT[:, :, j * 128:(j + 1) * 128], in_=psT[s][:, :].rearrange("p (c q) -> p c q", c=4)),
                 reads=[r_psT[s]], writes=[r_oT[j]])
        for dc in range(KC):
            s = dc % 2
            for fc in range(4):
                P.op("pe", lambda e: e.matmul(psO[s][:, :], lhsT=wos[:, fc, dc * 128:(dc + 1) * 128], rhs=oT[:, fc, :],
                                              start=(fc == 0), stop=(fc == 3)),
                     reads=[r_wo[fc]] + r_oT, writes=[r_psO[s]], signal=(fc == 3))
            P.op("dve", lambda e: e.tensor_tensor(out=hb[b][:, dc, :], in0=psO[s][:, :], in1=hb[b][:, dc, :], op=ALU.add),
                 reads=[r_psO[s], r_hb[b][dc]], writes=[r_hb[b][dc]])
        P.dma("sp", C.dst[b], hv[:, :, i * TT:(i + 1) * TT], hb[b][:, :, :], reads=r_hb[b])
    P.barrier()


def gla_phase(P, C, W, gcol):
    sb = C.sb
    hv = C.hs.rearrange("(k p) s -> p k s", p=128)
    NCH = S // 128
    for pas in range(int(os.environ.get("GLA_PASSES", "2"))):
        dirn = 1 - pas
        last = (pas == 1)
        sb.reset()
        win = sb.t([128, KC, 3072], BF16)
        wa1 = sb.t([128, KC, 16], BF16)
        wa2a = sb.t([17, 512], BF16)
        GLt = sb.t([128, 4, 128], F32)
        GMt = sb.t([128, 2, 4, 128], F32)
        hb = [sb.t([128, KC, TT], F32) for _ in range(2)]
        hn_l = [sb.t([128, KC, TT], BF16) for _ in range(2)]
        sq = [sb.t([128, TT], BF16)[:, :] for _ in range(4)]
        rstd = sb.t([128, TT], F32)
        qT = sb.t([128, 4, TT], F32)
        kT = sb.t([128, 4, TT], F32)
        t1a = sb.t([32, TT], BF16)
        Sm = sb.t([128, 4, 256], F32)
        Sb = sb.t([128, 4, 256], BF16)
        lt = sb.t([128, 512], F32)
        lg = sb.t([128, 512], F32)
        Eq = sb.t([128, 512], F32)
        Ek = sb.t([128, 512], F32)
        eks = sb.t([128, 512], F32)
        qtl = sb.t([128, 4, 128], BF16)
        ktl = sb.t([128, 4, 128], BF16)
        ks = sb.t([128, 512], BF16)
        vt = sb.t([128, 1024], BF16)
        sT = sb.t([128, 4, 128], BF16)
        osb = [sb.t([128, 4, 256], F32) for _ in range(2)]
        if last:
            wos = sb.t([128, KC, D], BF16)
            ident = sb.t([128, 128], BF16)
            ngb = sb.t([128, 1024], F32)
            obl = [sb.t([128, 1024], F32) for _ in range(2)]
            sr = sb.t([128, 1024], F32)
            junk = sb.t([128, 4, 256], F32)
            ssq = sb.t([128, 4, 1], F32)
            yb = sb.t([128, 1024], BF16)
            yT = sb.t([128, KC, TT], BF16)
        order = list(range(NT)) if dirn == 0 else list(range(NT - 1, -1, -1))
        r_hb = [P.res(KC) for _ in range(2)]
        P.dma("sp", C.dh[order[0] % 2], hb[order[0] % 2][:, :, :], hv[:, :, order[0] * TT:(order[0] + 1) * TT], writes=r_hb[order[0] % 2])
        r_win, r_wa1 = P.res(KC), P.res()
        load_weight(P, C, win, W["winb"].rearrange("(k p) f -> p k f", p=128), KC, r_win, per=2)
        P.dma("sp", C.dm[0], wa1[:, :, :], W["wa1b"][dirn].rearrange("(k p) f -> p k f", p=128), writes=[r_wa1], reads=[C.cast_res["wa1b"]])
        r_wa2, r_tab = P.res(), P.res()
        P.dma("pool", C.dm[1], wa2a[0:16, :], C.ext["gla_w_a2"][0, dirn], writes=[r_wa2])
        P.dma("pool", C.dm[2], wa2a[16:17, :], C.ext["gla_b_a"][0, dirn:dirn + 1, :], writes=[r_wa2])
        P.dma("sp", C.dm[3], GLt[:, :, :].rearrange("p a b -> p (a b)"), C.tabs["GL"], writes=[r_tab])
        P.dma("sp", C.dm[0], GMt[:, :, :, :].rearrange("p a b c -> p (a b c)"), C.tabs["GM"], writes=[r_tab])
        r_S, r_Sb = P.res(4), P.res(4)
        r_t1a = P.res()
        P.op("pool", lambda e: e.memset(Sm[:, :, :], 0.0), writes=r_S)
        P.op("pool", lambda e: e.memset(Sb[:, :, :], 0.0), writes=r_Sb)
        P.op("pool", lambda e: e.memset(t1a[:, :], 1.0), writes=[r_t1a])
        if last:
            r_wo, r_id, r_ng = P.res(KC), P.res(), P.res()
            load_weight(P, C, wos, W["gwob"].rearrange("(k p) f -> p k f", p=128), KC, r_wo, per=4)
            P.dma("sp", C.dm[1], ident[:, :], C.tabs["ID"], writes=[r_id])
            P.dma("sp", C.dm[2], ngb[:, :], C.ngd, writes=[r_ng])
            r_obl, r_sr, r_junk, r_ssq, r_yb = P.res(2), P.res(), P.res(), P.res(), P.res()
            r_yT = P.res(4)
            r_psT = P.res(2)
        r_hn_l = [P.res(KC) for _ in range(2)]
        r_sq = P.res(4)
        r_st, r_rstd = P.res(), P.res()
        r_qT, r_kT = P.res(), P.res()
        r_lt, r_lg, r_Eq, r_Ek, r_eks, r_qtl, r_ktl, r_ks, r_vt, r_sT = (P.res() for _ in range(10))
        r_osb = P.res(2)
        r_bank = P.res(6)
        bk = [0]

        def bank():
            i = bk[0] % 6
            bk[0] += 1
            return C.ps[i], r_bank[i]

        def loadh(i):
            b = i % 2
            P.dma("sp", C.dh[b], hb[b][:, :, :], hv[:, :, i * TT:(i + 1) * TT], writes=r_hb[b])

        pendT = []

        def flushT():
            while pendT:
                jj = pendT.pop(0)
                tj = slice(jj * 128, (jj + 1) * 128)
                for half in range(2):
                    pt_, rt_ = bank()
                    for q in range(4):
                        fc = half * 4 + q
                        P.op("pe", lambda e: e.matmul(pt_[:, q * 128:(q + 1) * 128], lhsT=yb[:, fc * 128:(fc + 1) * 128], rhs=ident[:, :],
                                                      start=True, stop=True),
                             reads=[r_yb, r_id], writes=[rt_], signal=(q == 3))
                    src = pt_[:, :].rearrange("p (c q) -> p c q", c=4)
                    if half == 0:
                        P.op("act", lambda e: e.copy(out=yT[:, 0:4, tj], in_=src), reads=[rt_], writes=[r_yT[jj]])
                    else:
                        P.op("dve", lambda e: e.tensor_copy(out=yT[:, 4:8, tj], in_=src), reads=[rt_], writes=[r_yT[jj]])

        un = 0
        for oi, i in enumerate(order):
            b = i % 2
            if oi + 1 < NT:
                loadh(order[oi + 1])
            hn, r_hn = hn_l[oi % 2], r_hn_l[oi % 2]
            if oi == 0:
                emit_stats(P, C, hb[b], r_hb[b], sq, r_sq, r_st)
                emit_norm(P, C, hb[b], r_hb[b], gcol, rstd, r_rstd, r_st, hn, r_hn)
            for t in range(2):
                dstT, r_dst = (qT, r_qT) if t == 0 else (kT, r_kT)
                for hd in range(4):
                    pb, rb = bank()
                    col = t * 512 + hd * 128
                    for k in range(KC):
                        P.op("pe", lambda e: e.matmul(pb[:, :], lhsT=win[:, k, col:col + 128], rhs=hn[:, k, :], start=(k == 0), stop=(k == KC - 1)),
                             reads=[r_win[k], r_hn[k]], writes=[rb], signal=(k == KC - 1))
                    if hd % 2 == 0:
                        P.op("act", lambda e: e.copy(out=dstT[:, hd, :], in_=pb[:, :]), reads=[rb], writes=[r_dst])
                    else:
                        P.op("dve", lambda e: e.tensor_copy(out=dstT[:, hd, :], in_=pb[:, :]), reads=[rb], writes=[r_dst])
            pb, rb = bank()
            for k in range(KC):
                P.op("pe", lambda e: e.matmul(pb[0:16, :], lhsT=wa1[:, k, :], rhs=hn[:, k, :], start=(k == 0), stop=(k == KC - 1)),
                     reads=[r_wa1, r_hn[k]], writes=[rb], signal=(k == KC - 1))
            P.op("act", lambda e: e.copy(out=t1a[0:16, :], in_=pb[0:16, :]), reads=[rb], writes=[r_t1a])
            jl = list(range(4)) if dirn == 0 else [3, 2, 1, 0]
            for jn, j in enumerate(jl):
                if jn == 2 and oi + 1 < NT:
                    bn = order[oi + 1] % 2
                    emit_stats(P, C, hb[bn], r_hb[bn], sq, r_sq, r_st)
                    emit_norm(P, C, hb[bn], r_hb[bn], gcol, rstd, r_rstd, r_st, hn_l[(oi + 1) % 2], r_hn_l[(oi + 1) % 2])
                ch = i * 4 + j
                tsl = slice(j * 128, (j + 1) * 128)
                if last:
                    ob_ = un % 2
                    P.dma("sp", C.dm[ob_], obl[ob_][:, :], C.OB[ch * 128:(ch + 1) * 128, :], writes=[r_obl[ob_]])
                pz, rz_ = bank()
                P.op("pe", lambda e: e.matmul(pz[:, :], lhsT=t1a[0:17, tsl], rhs=wa2a[:, :], start=True, stop=True),
                     reads=[r_t1a, r_wa2], writes=[rz_])
                P.op("act", lambda e: e.activation(out=lt[:, :], in_=pz[:, :], func=AF.Exp, scale=-1.0), reads=[rz_], writes=[r_lt])
                P.op("act", lambda e: e.activation(out=lg[:, :], in_=lt[:, :], func=AF.Ln, bias=C.one_col[:, 0:1]), reads=[r_lt], writes=[r_lg])
                pk, rk_ = bank()
                for k in range(KC):
                    P.op("pe", lambda e: e.matmul(pk[:, :], lhsT=hn[:, k, tsl], rhs=win[:, k, 512:1024], start=(k == 0), stop=(k == KC - 1)),
                         reads=[r_win[k], r_hn[k]], writes=[rk_], signal=(k == KC - 1))
                for hv_ in range(2):
                    pv, rv_ = bank()
                    for k in range(KC):
                        P.op("pe", lambda e: e.matmul(pv[:, :], lhsT=hn[:, k, tsl], rhs=win[:, k, 1024 + hv_ * 512:1536 + hv_ * 512],
                                                      start=(k == 0), stop=(k == KC - 1)),
                             reads=[r_win[k], r_hn[k]], writes=[rv_], signal=(k == KC - 1))
                    P.op("act", lambda e: e.copy(out=vt[:, hv_ * 512:(hv_ + 1) * 512], in_=pv[:, :]), reads=[rv_], writes=[r_vt])
                pc, rc_ = bank()
                for h in range(4):
                    P.op("pe", lambda e: e.matmul(pc[:, h * 128:(h + 1) * 128], lhsT=lg[:, h * 128:(h + 1) * 128], rhs=GLt[:, dirn, :],
                                                  start=True, stop=True), reads=[r_lg, r_tab], writes=[rc_], signal=(h == 3))
                pd, rd_ = bank()
                P.op("pe", lambda e: e.matmul(pd[:, :], lhsT=GLt[:, 2 + dirn, :], rhs=lg[:, :], start=True, stop=True),
                     reads=[r_lg, r_tab], writes=[rd_])
                P.op("act", lambda e: e.activation(out=Eq[:, :], in_=pc[:, :], func=AF.Exp), reads=[rc_], writes=[r_Eq])
                P.op("act", lambda e: e.activation(out=Ek[:, :], in_=pc[:, :], func=AF.Exp, scale=-1.0), reads=[rc_], writes=[r_Ek])
                P.op("act", lambda e: e.activation(out=eks[:, :], in_=pd[:, :], func=AF.Exp), reads=[rd_], writes=[r_eks])
                P.op("dve", lambda e: e.scalar_tensor_tensor(out=qtl[:, :, :], in0=qT[:, :, tsl], scalar=128.0 ** -0.5,
                                                             in1=Eq[:, :].rearrange("p (h t) -> p h t", h=4), op0=ALU.mult, op1=ALU.mult),
                     reads=[r_qT, r_Eq], writes=[r_qtl])
                P.op("dve", lambda e: e.tensor_tensor(out=ktl[:, :, :], in0=kT[:, :, tsl], in1=Ek[:, :].rearrange("p (h t) -> p h t", h=4),
                                                      op=ALU.mult), reads=[r_kT, r_Ek], writes=[r_ktl])
                P.op("dve", lambda e: e.tensor_tensor(out=ks[:, :], in0=pk[:, :], in1=eks[:, :], op=ALU.mult), reads=[rk_, r_eks], writes=[r_ks])
                if last:
                    for hv_ in range(2):
                        pr, rr_ = bank()
                        for k in range(KC):
                            P.op("pe", lambda e: e.matmul(pr[:, :], lhsT=hn[:, k, tsl], rhs=win[:, k, 2048 + hv_ * 512:2560 + hv_ * 512],
                                                          start=(k == 0), stop=(k == KC - 1)),
                                 reads=[r_win[k], r_hn[k]], writes=[rr_], signal=(k == KC - 1))
                        P.op("act", lambda e: e.activation(out=sr[:, hv_ * 512:(hv_ + 1) * 512], in_=pr[:, :], func=AF.Silu), reads=[rr_], writes=[r_sr])
                psc, rsc = bank()
                for h in range(4):
                    P.op("pe", lambda e: e.matmul(psc[:, h * 128:(h + 1) * 128], lhsT=ktl[:, h, :], rhs=qtl[:, h, :], start=True, stop=True),
                         reads=[r_ktl, r_qtl], writes=[rsc], signal=(h == 3))
                P.op("dve", lambda e: e.tensor_tensor(out=sT[:, :, :].rearrange("p h t -> p (h t)"), in0=psc[:, :],
                                                      in1=GMt[:, dirn, :, :].rearrange("p h t -> p (h t)"), op=ALU.mult),
                     reads=[rsc, r_tab], writes=[r_sT])
                po = [bank(), bank()]
                for h in range(4):
                    pb_, rb_ = po[h // 2]
                    osl = slice((h % 2) * 256, (h % 2) * 256 + 256)
                    P.op("pe", lambda e: e.matmul(pb_[:, osl], lhsT=sT[:, h, :], rhs=vt[:, h * 256:(h + 1) * 256], start=True, stop=False),
                         reads=[r_sT, r_vt], writes=[rb_], signal=False)
                    P.op("pe", lambda e: e.matmul(pb_[:, osl], lhsT=qtl[:, h, :], rhs=Sb[:, h, :], start=False, stop=True),
                         reads=[r_qtl, r_Sb[h]], writes=[rb_], signal=(h % 2 == 1))
                tl = 127 if dirn == 0 else 0
                pst = [bank(), bank()]
                for h in range(4):
                    pb_, rb_ = pst[h // 2]
                    osl = slice((h % 2) * 256, (h % 2) * 256 + 256)
                    P.op("pe", lambda e: e.matmul(pb_[:, osl], lhsT=ks[:, h * 128:(h + 1) * 128], rhs=vt[:, h * 256:(h + 1) * 256], start=True, stop=True),
                         reads=[r_ks, r_vt], writes=[rb_])
                    P.op("dve", lambda e: e.scalar_tensor_tensor(out=Sm[:, h, :], in0=Sm[:, h, :], scalar=Eq[:, h * 128 + tl:h * 128 + tl + 1],
                                                                 in1=pb_[:, osl], op0=ALU.mult, op1=ALU.add),
                         reads=[r_S[h], r_Eq, rb_], writes=[r_S[h]])
                    P.op("act", lambda e: e.copy(out=Sb[:, h, :], in_=Sm[:, h, :]), reads=[r_S[h]], writes=[r_Sb[h]])
                if last:
                    flushT()
                so = un % 2
                osf = osb[so][:, :, :].rearrange("p h v -> p (h v)")
                if not last:
                    P.op("act", lambda e: e.copy(out=osf[:, 0:512], in_=po[0][0][:, :]), reads=[po[0][1]], writes=[r_osb[so]])
                    P.op("dve", lambda e: e.tensor_copy(out=osf[:, 512:1024], in_=po[1][0][:, :]), reads=[po[1][1]], writes=[r_osb[so]])
                    P.dma("sp", C.dst[so], C.OB[ch * 128:(ch + 1) * 128, :], osf, reads=[r_osb[so]])
                else:
                    for q in range(2):
                        P.op("dve", lambda e: e.tensor_tensor(out=osf[:, q * 512:(q + 1) * 512], in0=po[q][0][:, :],
                                                              in1=obl[ob_][:, q * 512:(q + 1) * 512], op=ALU.add),
                             reads=[po[q][1], r_obl[ob_]], writes=[r_osb[so]])
                    LV2 = int(os.environ.get("GLA_L2", "9"))
                    if LV2 < 1:
                        un += 1
                        continue
                    P.op("act", lambda e: e.activation(out=junk[:, :, :], in_=osb[so][:, :, :], func=AF.Square),
                         reads=[r_osb[so]], writes=[r_junk])
                    P.op("dve", lambda e: e.reduce_sum(out=ssq[:, :, :], in_=junk[:, :, :], axis=mybir.AxisListType.X),
                         reads=[r_junk], writes=[r_ssq])
                    P.op("act", lambda e: e.activation(out=ssq[:, :, :], in_=ssq[:, :, :], func=AF.Sqrt, scale=1.0 / 256, bias=C.eps_col[:, 0:1]),
                         reads=[r_ssq], writes=[r_ssq])
                    P.op("dve", lambda e: e.reciprocal(out=ssq[:, :, :], in_=ssq[:, :, :]), reads=[r_ssq], writes=[r_ssq])
                    P.op("dve", lambda e: e.tensor_tensor(out=osb[so][:, :, :], in0=osb[so][:, :, :],
                                                          in1=ssq[:, :, :].to_broadcast([128, 4, 256]), op=ALU.mult),
                         reads=[r_osb[so], r_ssq], writes=[r_osb[so]])
                    P.op("dve", lambda e: e.tensor_tensor(out=osf, in0=osf, in1=ngb[:, :], op=ALU.mult), reads=[r_osb[so], r_ng], writes=[r_osb[so]])
                    P.op("dve", lambda e: e.tensor_tensor(out=yb[:, :], in0=osf, in1=sr[:, :], op=ALU.mult), reads=[r_osb[so], r_sr], writes=[r_yb])
                    pendT.append(j)
                un += 1
            if last and int(os.environ.get("GLA_L2", "9")) >= 4:
                flushT()
                for dc in range(KC):
                    po_, ro_ = bank()
                    for fc in range(KC):
                        P.op("pe", lambda e: e.matmul(po_[:, :], lhsT=wos[:, fc, dc * 128:(dc + 1) * 128], rhs=yT[:, fc, :],
                                                      start=(fc == 0), stop=(fc == KC - 1)),
                             reads=[r_wo[fc]] + r_yT, writes=[ro_], signal=(fc == KC - 1))
                    P.op("dve", lambda e: e.tensor_tensor(out=hb[b][:, dc, :], in0=po_[:, :], in1=hb[b][:, dc, :], op=ALU.add),
                         reads=[ro_, r_hb[b][dc]], writes=[r_hb[b][dc]])
                P.dma("sp", C.dst[b], hv[:, :, i * TT:(i + 1) * TT], hb[b][:, :, :], reads=r_hb[b])
        P.barrier()

def _bf16(a):
    import ml_dtypes
    return np.ascontiguousarray(np.asarray(a, dtype=np.float32).astype(ml_dtypes.bfloat16))


def make_tables():
    T = {}
    a = np.arange(64)[:, None]
    c = np.arange(64)[None, :]
    al = 2 * np.pi * ((a * c) % 64) / 64.0
    T["F1"] = _bf16(np.concatenate([np.cos(al), np.sin(al), -np.sin(al)], axis=1))
    b = np.arange(128)[:, None, None]
    cc = np.arange(64)[None, :, None]
    d = np.arange(128)[None, None, :]
    ph = 2 * np.pi * ((b * (64 * d + cc)) % 8192) / 8192.0
    T["Mt"] = _bf16(np.stack([np.cos(ph), np.sin(ph)], axis=2))
    n2 = np.arange(256)[:, None]
    k2 = np.arange(256)[None, :]
    th = 2 * np.pi * ((n2 * k2) % 256) / 256.0
    cs = np.stack([np.cos(th), -np.sin(th)], axis=1)
    T["CS"] = _bf16(cs.reshape(2, 128, 2, 256).transpose(1, 0, 2, 3))
    slopes = np.exp2(-8.0 * np.arange(1, 25, dtype=np.float64) / 24.0)
    kk = np.arange(128)[:, None]
    qq = np.arange(128)[None, :]
    ET = np.zeros((128, 3, 2, 8, 128), np.float64)
    for g in range(3):
        for AB in range(2):
            rel = (kk - 64 + AB * 128) - qq
            valid = np.abs(rel) <= 64
            for h in range(8):
                ET[:, g, AB, h, :] = np.where(valid, np.exp(-slopes[g * 8 + h] * DIL[g] * np.abs(rel)), 0.0)
    T["EA"] = np.ascontiguousarray(ET.reshape(128, -1).astype(np.float32))
    T["ID"] = _bf16(np.eye(128))
    ss_ = np.arange(128)[:, None]
    tt_ = np.arange(128)[None, :]
    GLm = np.stack([(ss_ <= tt_), (ss_ >= tt_), (ss_ > tt_), (ss_ < tt_)], axis=1).astype(np.float64) * (-1.0 / 16.0)
    T["GL"] = np.ascontiguousarray(GLm.reshape(128, 512).astype(np.float32))
    m0 = (ss_ <= tt_).astype(np.float32)
    m1 = (ss_ > tt_).astype(np.float32)
    GM = np.stack([np.repeat(m0[:, None, :], 4, axis=1), np.repeat(m1[:, None, :], 4, axis=1)], axis=1)
    T["GM"] = np.ascontiguousarray(GM.reshape(128, 1024).astype(np.float32))
    return T


TABLE_SHAPES = {"F1": ([64, 192], BF16), "Mt": ([128, 64, 2, 128], BF16), "CS": ([128, 2, 2, 256], BF16),
                "EA": ([128, 3 * 2 * 8 * 128], F32), "ID": ([128, 128], BF16), "GL": ([128, 512], F32), "GM": ([128, 1024], F32)}

COLS = {
    "norm_g": 96, "final_norm_g": 8, "conv_b_pw1": 16, "conv_w_dw": 8 * 31, "conv_b_dw": 8, "conv_ln_g": 8, "conv_ln_b": 8,
    "conv_b_pw2": 8, "fnet_b": 8,
}


def make_cols(inp):
    def col(v):
        v = np.asarray(v, np.float32)
        lead = int(np.prod(v.shape[:-1])) if v.ndim > 1 else 1
        return np.ascontiguousarray(v.reshape(lead, -1, 128).transpose(2, 0, 1).reshape(128, -1))
    out = {}
    out["norm_g"] = col(inp["norm_g"])
    out["final_norm_g"] = col(inp["final_norm_g"])
    out["conv_b_pw1"] = col(inp["conv_b_pw1"][0])
    wdw = np.asarray(inp["conv_w_dw"][0], np.float32)
    out["conv_w_dw"] = np.ascontiguousarray(wdw.reshape(31, 8, 128).transpose(2, 1, 0).reshape(128, 8 * 31))
    for k in ("conv_b_dw", "conv_ln_g", "conv_ln_b", "conv_b_pw2", "fnet_b"):
        out[k] = col(inp[k][0])
    return out


WEIGHTS = {
    "pw1b": ("conv_w_pw1", (0,), [D, 2 * D]), "pw2b": ("conv_w_pw2", (0,), [D, D]), "wfb": ("fnet_w", (0,), [D, D]),
    "wqkvb": ("attn_w_qkv", (0,), [D, 4608]), "wob": ("attn_w_o", (0,), [512, D]),
    "winb": ("gla_w_in", (0,), [D, 3072]), "gwob": ("gla_w_o", (0,), [D, D]), "wa1b": ("gla_w_a1", (0,), [2, D, 16]),
}


def build_program(layers=(0, 1, 2, 3), do_ffn=True, do_mixer=True, do_final=True):
    nc = bass.Bass("TRN2", target_bir_lowering=False)
    C = Ctx()
    P = Prog(nc)
    xT = nc.dram_tensor("xT", [D, S], F32, kind="ExternalInput").ap()
    C.outT = nc.dram_tensor("outT", [D, S], F32, kind="ExternalOutput").ap()
    ext = {}
    ext["ffn_w1"] = nc.dram_tensor("ffn_w1", [4, 2, D, FF], F32, kind="ExternalInput").ap()
    ext["ffn_w3"] = nc.dram_tensor("ffn_w3", [4, 2, D, FF], F32, kind="ExternalInput").ap()
    ext["ffn_w2"] = nc.dram_tensor("ffn_w2", [4, 2, FF, D], F32, kind="ExternalInput").ap()
    ext["conv_w_pw1"] = nc.dram_tensor("conv_w_pw1", [1, D, 2 * D], F32, kind="ExternalInput").ap()
    ext["conv_w_pw2"] = nc.dram_tensor("conv_w_pw2", [1, D, D], F32, kind="ExternalInput").ap()
    ext["fnet_w"] = nc.dram_tensor("fnet_w", [1, D, D], F32, kind="ExternalInput").ap()
    ext["attn_w_qkv"] = nc.dram_tensor("attn_w_qkv", [1, D, 4608], F32, kind="ExternalInput").ap()
    ext["attn_w_o"] = nc.dram_tensor("attn_w_o", [1, 512, D], F32, kind="ExternalInput").ap()
    ext["gla_w_in"] = nc.dram_tensor("gla_w_in", [1, D, 3072], F32, kind="ExternalInput").ap()
    ext["gla_w_o"] = nc.dram_tensor("gla_w_o", [1, D, D], F32, kind="ExternalInput").ap()
    ext["gla_w_a1"] = nc.dram_tensor("gla_w_a1", [1, 2, D, 16], F32, kind="ExternalInput").ap()
    ext["gla_w_a2"] = nc.dram_tensor("gla_w_a2", [1, 2, 16, 512], F32, kind="ExternalInput").ap()
    ext["gla_b_a"] = nc.dram_tensor("gla_b_a", [1, 2, 512], F32, kind="ExternalInput").ap()
    C.ngd = nc.dram_tensor("c_gla_ng", [128, 1024], F32, kind="ExternalInput").ap()
    C.ext = ext
    colsd = {k: nc.dram_tensor("c_" + k, [128, n], F32, kind="ExternalInput").ap() for k, n in COLS.items()}
    C.tabs = {k: nc.dram_tensor("t_" + k, shp, dt, kind="ExternalInput").ap() for k, (shp, dt) in TABLE_SHAPES.items()}
    C.hs = nc.dram_tensor("hs", [D, S], F32).ap()
    C.GL = nc.dram_tensor("GL", [D, S + 32], BF16).ap()
    C.UT = nc.dram_tensor("UT", [D, S], BF16).ap()
    C.PT = nc.dram_tensor("PT", [D, S], BF16).ap()
    C.QT = nc.dram_tensor("QT", [D, S], BF16).ap()
    dk = "ExternalOutput" if os.environ.get("DBG") else "Internal"
    C.AQ = [nc.dram_tensor(f"AQ{g}", [512, S], BF16, kind=dk).ap() for g in range(3)]
    C.KT = [nc.dram_tensor(f"KT{g}", [512, S + 2 * PADK], BF16, kind=dk).ap() for g in range(3)]
    C.VA = [nc.dram_tensor(f"VA{g}", [S + 2 * PADK, 520], BF16, kind=dk).ap() for g in range(3)]
    C.NZ = [nc.dram_tensor(f"NZ{g}", [S, 520], F32, kind=dk).ap() for g in range(3)]
    C.OB = nc.dram_tensor("OB", [S, 1024], F32, kind=dk).ap()
    wb = {}
    for l in range(4):
        for j in range(2):
            wb[("w1", l, j)] = nc.dram_tensor(f"w1b_{l}_{j}", [D, FF], BF16).ap()
            wb[("w3", l, j)] = nc.dram_tensor(f"w3b_{l}_{j}", [D, FF], BF16).ap()
            wb[("w2", l, j)] = nc.dram_tensor(f"w2b_{l}_{j}", [FF, D], BF16).ap()
    W = {k: nc.dram_tensor(k, shp, BF16).ap() for k, (_, _, shp) in WEIGHTS.items()}
    C.ones_b = nc.alloc_sbuf_tensor("ones_b", [128, 128], BF16)
    C.ones_f = nc.alloc_sbuf_tensor("ones_f", [128, 128], F32)
    C.eps_col = nc.alloc_sbuf_tensor("eps_col", [128, 1], F32)
    C.one_col = nc.alloc_sbuf_tensor("one_col", [128, 1], F32)
    C.cols = {k: nc.alloc_sbuf_tensor("col_" + k, [128, n], F32) for k, n in COLS.items()}
    arena_bytes = nc.sbuf_bytes_remaining - 64
    arena = nc.alloc_sbuf_tensor("arena", [128, arena_bytes // 4], F32)
    base = nc.lookup_mloc(arena).addr
    C.sb = SB(nc, base, base + (arena_bytes // 4) * 4)
    C.ps = [nc.alloc_psum_tensor(f"ps{i}", [128, 512], F32) for i in range(6)]
    psb_t = nc.alloc_psum_tensor("psb", [128, 1024], BF16)
    C.psb = [psb_t[:, 0:512], psb_t[:, 512:1024]]
    C.psS = nc.alloc_psum_tensor("psS", [128, 512], F32)
    C.dw = [P.dstream(f"w{i}") for i in range(12)]
    C.dwi = 0
    C.dh = [P.dstream(f"h{i}") for i in range(2)]
    C.dst = [P.dstream(f"st{i}") for i in range(2)]
    C.dm = [P.dstream(f"m{i}") for i in range(4)]
    dcast = [P.dstream(f"c{i}", barrier=False) for i in range(6)]
    r0 = P.res()
    P.op("dve", lambda e: e.memset(C.ones_b[:, :], 1.0), writes=[r0])
    P.op("dve", lambda e: e.memset(C.ones_f[:, :], 1.0), writes=[r0])
    P.op("dve", lambda e: e.memset(C.eps_col[:, :], EPS), writes=[r0])
    P.op("dve", lambda e: e.memset(C.one_col[:, :], 1.0), writes=[r0])
    for i, (k, n) in enumerate(COLS.items()):
        P.dma("sp", C.dm[i % 4], C.cols[k][:, :], colsd[k], writes=[r0])
    ci = 0
    C.cast_res = {}

    def cast(dst, src):
        nonlocal ci
        r = P.res()
        C.cast_res[dst.tensor.name] = r
        P.dma("pool", dcast[ci % len(dcast)], dst, src, writes=[r])
        ci += 1

    def cast_ffn(l, j):
        cast(wb[("w1", l, j)], ext["ffn_w1"][l, j])
        cast(wb[("w3", l, j)], ext["ffn_w3"][l, j])
        cast(wb[("w2", l, j)], ext["ffn_w2"][l, j])

    def cast_mixer(l):
        if l == 0:
            cast(W["wqkvb"], ext["attn_w_qkv"][0])
            cast(W["wob"], ext["attn_w_o"][0])
        if l == 1:
            cast(W["pw1b"], ext["conv_w_pw1"][0])
            cast(W["pw2b"], ext["conv_w_pw2"][0])
        if l == 2:
            cast(W["wfb"], ext["fnet_w"][0])
        if l == 3:
            cast(W["winb"], ext["gla_w_in"][0])
            cast(W["wa1b"], ext["gla_w_a1"][0])
            cast(W["gwob"], ext["gla_w_o"][0])

    gn = C.cols["norm_g"]
    phases = []
    for l in layers:
        if do_ffn:
            phases.append(("ffn", l, 0))
        if do_mixer:
            phases.append(("mix", l, 0))
        if do_ffn:
            phases.append(("ffn", l, 1))

    def do_cast(ph):
        if ph[0] == "ffn":
            cast_ffn(ph[1], ph[2])
        else:
            cast_mixer(ph[1])

    if phases:
        do_cast(phases[0])
    P.barrier()
    first = True
    for pi, ph in enumerate(phases):
        if pi + 1 < len(phases):
            do_cast(phases[pi + 1])
        kind, l, j = ph
        src = xT if first else None
        if kind == "ffn":
            gi = (l * 3 + (0 if j == 0 else 2)) * 8
            ffn_phase(P, C, wb[("w1", l, j)], wb[("w3", l, j)], wb[("w2", l, j)], gn[:, gi:gi + 8], hsrc=src)
        else:
            if first:
                P.dma("sp", C.dh[0], C.hs, xT, writes=[r0])
                P.barrier()
            g1 = gn[:, (l * 3 + 1) * 8:(l * 3 + 2) * 8]
            if l == 0:
                attn_phase(P, C, W, g1)
            elif l == 1:
                conv_phase(P, C, W, g1)
            elif l == 2:
                fourier_phase(P, C, W, g1)
            elif l == 3:
                gla_phase(P, C, W, g1)
        first = False
    if not phases:
        P.dma("sp", C.dh[0], C.hs, xT, writes=[r0])
        P.barrier()
    if do_final:
        final_phase(P, C, C.cols["final_norm_g"])
    else:
        P.dma("sp", C.dh[0], C.outT, C.hs)
    P.finish()
    C.P = P
    return nc, C


def make_in_maps(inp, ncores=NCORES):
    x = np.asarray(inp["x"], np.float32)
    cols = make_cols(inp)
    tabs = make_tables()
    shared = {}
    for k in ("ffn_w1", "ffn_w3", "ffn_w2", "conv_w_pw1", "conv_w_pw2", "fnet_w", "attn_w_qkv", "attn_w_o",
              "gla_w_in", "gla_w_o", "gla_w_a1", "gla_w_a2", "gla_b_a"):
        shared[k] = np.ascontiguousarray(np.asarray(inp[k], np.float32))
    for k, v in cols.items():
        shared["c_" + k] = v
    shared["c_gla_ng"] = np.ascontiguousarray(np.broadcast_to(np.asarray(inp["gla_norm_g"], np.float32).reshape(1, 1024), (128, 1024)))
    for k, v in tabs.items():
        shared["t_" + k] = v
    maps = []
    for b in range(ncores):
        m = dict(shared)
        m["xT"] = np.ascontiguousarray(x[b].T)
        maps.append(m)
    return maps


def kernel(**inputs):
    nc, C = build_program()
    in_maps = make_in_maps(inputs)
    res = run_bass_kernel_spmd(nc, in_maps, core_ids=list(range(NCORES)))
    out = np.stack([np.asarray(r["outT"]).T for r in res.results], axis=0)
    return np.ascontiguousarray(out.astype(np.float32))
```

```python
import os
import math
import numpy as np
import concourse.bass as bass
import concourse.mybir as mybir
from concourse.bass_utils import run_bass_kernel_spmd

F32 = mybir.dt.float32
BF16 = mybir.dt.bfloat16
ALU = mybir.AluOpType
AF = mybir.ActivationFunctionType

D = 1024
S = 8192
FF = 2816
KC = 8
FC = 22
TT = 512
NT = S // TT
EPS = 1e-6
NCORES = 8
DIL = (1, 4, 16)
WIN = (128, 512, 2048)


class Res:
    __slots__ = ("lw", "rd")

    def __init__(self):
        self.lw = None
        self.rd = {}


class Stream:
    __slots__ = ("name", "sem", "cnt")

    def __init__(self, name, sem):
        self.name = name
        self.sem = sem
        self.cnt = 0


class Prog:
    ENG = ("pe", "act", "dve", "pool", "sp")

    def __init__(self, nc):
        self.nc = nc
        self.e = {"pe": nc.tensor, "act": nc.scalar, "dve": nc.vector, "pool": nc.gpsimd, "sp": nc.sync}
        self.es = {k: Stream(k, nc.alloc_semaphore(name="s_" + k)) for k in self.ENG}
        self.seen = {k: {} for k in self.ENG}
        self.dstreams = []
        self.nobar = set()
        self.ninst = {k: 0 for k in self.ENG}

    def res(self, n=None):
        if n is None:
            return Res()
        return [Res() for _ in range(n)]

    def dstream(self, name, barrier=True):
        s = Stream(name, self.nc.alloc_semaphore(name="d_" + name))
        self.dstreams.append(s)
        if not barrier:
            self.nobar.add(s)
        return s

    def _wait(self, eng, stream, val):
        if val <= 0 or self.seen[eng].get(stream, 0) >= val:
            return
        self.e[eng].wait_ge(stream.sem, val)
        self.ninst[eng] += 1
        self.seen[eng][stream] = val

    def _deps(self, eng, reads, writes):
        deps = {}

        def add(t):
            if t is not None and deps.get(t[0], 0) < t[1]:
                deps[t[0]] = t[1]
        for r in reads:
            add(r.lw)
        for w in writes:
            add(w.lw)
            for s, v in w.rd.items():
                add((s, v))
        for s, v in deps.items():
            if eng == "pe" and s is self.es["pe"]:
                continue
            self._wait(eng, s, v)

    def op(self, eng, fn, reads=(), writes=(), signal=True):
        self._deps(eng, reads, writes)
        ins = fn(self.e[eng])
        self.ninst[eng] += 1
        st = self.es[eng]
        if signal:
            st.cnt += 1
            ins.then_inc(st.sem, 1)
            v = st.cnt
        else:
            v = st.cnt + 1
        for r in reads:
            if r.rd.get(st, 0) < v:
                r.rd[st] = v
        for w in writes:
            w.lw = (st, v)
            w.rd = {}
        return ins

    def dma(self, q, ds, out, in_, reads=(), writes=(), nowait=False, **kw):
        if not nowait:
            self._wait(q, ds, ds.cnt)
        self._deps(q, reads, writes)
        ins = self.e[q].dma_start(out=out, in_=in_, **kw)
        self.ninst[q] += 1
        ds.cnt += 16
        ins.then_inc(ds.sem, 16)
        v = ds.cnt
        for r in reads:
            if r.rd.get(ds, 0) < v:
                r.rd[ds] = v
        for w in writes:
            w.lw = (ds, v)
            w.rd = {}
        return ins

    def barrier(self):
        for e in self.ENG:
            for f in self.ENG:
                if f != e:
                    self._wait(e, self.es[f], self.es[f].cnt)
            for ds in self.dstreams:
                if ds not in self.nobar:
                    self._wait(e, ds, ds.cnt)

    def finish(self):
        for ds in self.dstreams:
            self._wait("sp", ds, ds.cnt)
        for f in self.ENG:
            if f != "sp":
                self._wait("sp", self.es[f], self.es[f].cnt)


class SB:
    def __init__(self, nc, base, limit):
        self.nc, self.base, self.limit = nc, base, limit
        self.off = base
        self.n = 0

    def reset(self):
        self.off = self.base

    def t(self, shape, dtype):
        esz = 2 if dtype == BF16 else 4
        nbytes = int(np.prod(shape[1:])) * esz
        off = (self.off + 63) // 64 * 64
        assert off + nbytes <= self.limit, f"SBUF overflow: need {off + nbytes - self.limit} more bytes"
        self.off = off + nbytes
        self.n += 1
        return self.nc.alloc_sbuf_tensor_at(f"sb{self.n}", list(shape), dtype, offset=off)


class Ctx:
    pass


def emit_stats(P, C, hbt, r_hb, sq, r_sq, r_st, part=None):
    if len(sq) >= KC:
        if part in (None, "sq"):
            for k in range(KC):
                P.op("act", lambda e: e.activation(out=sq[k], in_=hbt[:, k, :], func=AF.Square),
                     reads=[r_hb[k]], writes=[r_sq[k]])
        if part in (None, "mm"):
            for k in range(KC):
                P.op("pe", lambda e: e.matmul(C.psS[:, :], lhsT=C.ones_b[:, :], rhs=sq[k], start=(k == 0), stop=(k == KC - 1)),
                     reads=[r_sq[k]], writes=[r_st], signal=(k == KC - 1))
        return
    for k in range(KC):
        s = k % len(sq)
        P.op("act", lambda e: e.activation(out=sq[s], in_=hbt[:, k, :], func=AF.Square),
             reads=[r_hb[k]], writes=[r_sq[s]])
        P.op("pe", lambda e: e.matmul(C.psS[:, :], lhsT=C.ones_b[:, :], rhs=sq[s], start=(k == 0), stop=(k == KC - 1)),
             reads=[r_sq[s]], writes=[r_st])


def emit_norm(P, C, hbt, r_hb, gcol, rstd, r_rstd, r_st, hn, r_hn, out_f32=None):
    P.op("act", lambda e: e.activation(out=rstd[:, :], in_=C.psS[:, :], func=AF.Sqrt, scale=1.0 / D, bias=C.eps_col[:, 0:1]),
         reads=[r_st], writes=[r_rstd])
    P.op("dve", lambda e: e.reciprocal(out=rstd[:, :], in_=rstd[:, :]), reads=[r_rstd], writes=[r_rstd])
    for k in range(KC):
        P.op("dve", lambda e: e.scalar_tensor_tensor(out=hn[:, k, :], in0=hbt[:, k, :], scalar=gcol[:, k:k + 1],
                                                     in1=rstd[:, :], op0=ALU.mult, op1=ALU.mult),
             reads=[r_hb[k], r_rstd], writes=[r_hn[k]])


def load_weight(P, C, dst, src_view, nchunk, res_list, per=1):
    rd = [C.cast_res[src_view.tensor.name]]
    for c0 in range(0, nchunk, per):
        c1 = min(nchunk, c0 + per)
        ds = C.dw[(C.dwi) % len(C.dw)]
        C.dwi += 1
        P.dma("sp", ds, dst[:, c0:c1, :], src_view[:, c0:c1, :], writes=res_list[c0:c1], reads=rd)


def ffn_phase(P, C, w1b, w3b, w2b, gcol, hsrc=None):
    sb = C.sb
    sb.reset()
    w1s = sb.t([128, KC, FF], BF16)
    w3s = sb.t([128, KC, FF], BF16)
    w2s = sb.t([128, FC, D], BF16)
    hb = [sb.t([128, KC, TT], F32) for _ in range(2)]
    hn = sb.t([128, KC, TT], BF16)
    act = sb.t([128, FC, TT], BF16)
    tmp = [sb.t([128, TT], F32) for _ in range(2)]
    rstd = sb.t([128, TT], F32)
    hv = C.hs.rearrange("(k p) s -> p k s", p=128)
    hvs = hv if hsrc is None else hsrc.rearrange("(k p) s -> p k s", p=128)
    r_hb = [P.res(KC) for _ in range(2)]
    P.dma("sp", C.dh[0], hb[0][:, :, :], hvs[:, :, 0:TT], writes=r_hb[0])
    FB = [(0, 768), (768, 1536), (1536, 2176), (2176, 2816)]
    fblk = [0] * 6 + [1] * 6 + [2] * 5 + [3] * 5
    r_w1, r_w3, r_w2 = P.res(4), P.res(4), P.res(4)
    w1v = w1b.rearrange("(k p) f -> p k f", p=128)
    w3v = w3b.rearrange("(k p) f -> p k f", p=128)
    w2v = w2b.rearrange("(c p) d -> p c d", p=128)
    rd1, rd3, rd2 = [C.cast_res[w1b.tensor.name]], [C.cast_res[w3b.tensor.name]], [C.cast_res[w2b.tensor.name]]
    for bi, (f0, f1) in enumerate(FB):
        P.dma("sp", C.dw[bi], w1s[:, :, f0:f1], w1v[:, :, f0:f1], writes=[r_w1[bi]], reads=rd1)
        P.dma("sp", C.dw[4 + bi], w3s[:, :, f0:f1], w3v[:, :, f0:f1], writes=[r_w3[bi]], reads=rd3)
    for bi in range(4):
        P.dma("sp", C.dw[8 + bi], w2s[:, :, bi * 256:(bi + 1) * 256], w2v[:, :, bi * 256:(bi + 1) * 256], writes=[r_w2[bi]], reads=rd2)
    r_hn = P.res(KC)
    r_st, r_rstd = P.res(), P.res()
    r_psG, r_psU, r_psO, r_tmp = P.res(2), P.res(2), P.res(2), P.res(2)
    r_act = P.res(FC)
    psG, psU, psO = C.ps[0:2], C.ps[2:4], C.ps[4:6]

    def load(i):
        b = i % 2
        P.dma("sp", C.dh[b], hb[b][:, :, :], hvs[:, :, i * TT:(i + 1) * TT], writes=r_hb[b])

    def gu(i):
        for f in range(FC):
            s = f % 2
            for k in range(KC):
                P.op("pe", lambda e: e.matmul(psG[s][:, :], lhsT=w1s[:, k, f * 128:(f + 1) * 128], rhs=hn[:, k, :],
                                              start=(k == 0), stop=(k == KC - 1)),
                     reads=[r_w1[fblk[f]], r_hn[k]], writes=[r_psG[s]], signal=(k == KC - 1))
            for k in range(KC):
                P.op("pe", lambda e: e.matmul(psU[s][:, :], lhsT=w3s[:, k, f * 128:(f + 1) * 128], rhs=hn[:, k, :],
                                              start=(k == 0), stop=(k == KC - 1)),
                     reads=[r_w3[fblk[f]], r_hn[k]], writes=[r_psU[s]], signal=(k == KC - 1))
            P.op("act", lambda e: e.activation(out=tmp[s][:, :], in_=psG[s][:, :], func=AF.Silu),
                 reads=[r_psG[s]], writes=[r_tmp[s]])
            P.op("dve", lambda e: e.tensor_tensor(out=act[:, f, :], in0=tmp[s][:, :], in1=psU[s][:, :], op=ALU.mult),
                 reads=[r_tmp[s], r_psU[s]], writes=[r_act[f]])

    sq = [hn[:, k, :] for k in range(KC)]
    r_sq = r_hn

    def down(i):
        b = i % 2
        for dc in range(KC):
            if dc == 2 and i + 1 < NT:
                b1 = (i + 1) % 2
                emit_stats(P, C, hb[b1], r_hb[b1], sq, r_sq, r_st, part="mm")
                emit_norm(P, C, hb[b1], r_hb[b1], gcol, rstd, r_rstd, r_st, hn, r_hn)
            s = dc % 2
            for f in range(FC):
                P.op("pe", lambda e: e.matmul(psO[s][:, :], lhsT=w2s[:, f, dc * 128:(dc + 1) * 128], rhs=act[:, f, :],
                                              start=(f == 0), stop=(f == FC - 1)),
                     reads=[r_w2[dc // 2], r_act[f]], writes=[r_psO[s]], signal=(f == FC - 1))
            P.op("dve", lambda e: e.scalar_tensor_tensor(out=hb[b][:, dc, :], in0=psO[s][:, :], scalar=0.5,
                                                         in1=hb[b][:, dc, :], op0=ALU.mult, op1=ALU.add),
                 reads=[r_psO[s], r_hb[b][dc]], writes=[r_hb[b][dc]])

    def store(i):
        b = i % 2
        P.dma("sp", C.dst[b], hv[:, :, i * TT:(i + 1) * TT], hb[b][:, :, :], reads=r_hb[b])

    emit_stats(P, C, hb[0], r_hb[0], sq, r_sq, r_st)
    emit_norm(P, C, hb[0], r_hb[0], gcol, rstd, r_rstd, r_st, hn, r_hn)
    for i in range(NT):
        if i + 1 < NT:
            load(i + 1)
        gu(i)
        if i + 1 < NT:
            b1 = (i + 1) % 2
            emit_stats(P, C, hb[b1], r_hb[b1], sq, r_sq, r_st, part="sq")
        down(i)
        store(i)
    P.barrier()


def final_phase(P, C, gcol):
    sb = C.sb
    sb.reset()
    hb = [sb.t([128, KC, TT], F32) for _ in range(2)]
    ob = [sb.t([128, KC, TT], F32) for _ in range(2)]
    sq = [sb.t([128, TT], BF16)[:, :] for _ in range(8)]
    rstd = sb.t([128, TT], F32)
    hv = C.hs.rearrange("(k p) s -> p k s", p=128)
    ov = C.outT.rearrange("(k p) s -> p k s", p=128)
    r_hb = [P.res(KC) for _ in range(2)]
    r_ob = [P.res(KC) for _ in range(2)]
    r_sq = P.res(8)
    r_st, r_rstd = P.res(), P.res()
    P.dma("sp", C.dh[0], hb[0][:, :, :], hv[:, :, 0:TT], writes=r_hb[0])
    for i in range(NT):
        b = i % 2
        if i + 1 < NT:
            P.dma("sp", C.dh[1 - b], hb[1 - b][:, :, :], hv[:, :, (i + 1) * TT:(i + 2) * TT], writes=r_hb[1 - b])
        emit_stats(P, C, hb[b], r_hb[b], sq, r_sq, r_st)
        emit_norm(P, C, hb[b], r_hb[b], gcol, rstd, r_rstd, r_st, ob[b], r_ob[b])
        P.dma("sp", C.dst[b], ov[:, :, i * TT:(i + 1) * TT], ob[b][:, :, :], reads=r_ob[b])
    P.barrier()


def conv_phase(P, C, W, gcol):
    sb = C.sb
    hv = C.hs.rearrange("(k p) s -> p k s", p=128)
    GLv = C.GL.rearrange("(k p) s -> p k s", p=128)
    sb.reset()
    pw1s = sb.t([128, KC, 2 * D], BF16)
    hb = [sb.t([128, KC, TT], F32) for _ in range(2)]
    hn_l = [sb.t([128, KC, TT], BF16) for _ in range(2)]
    sq = [sb.t([128, TT], BF16)[:, :] for _ in range(8)]
    rstd = sb.t([128, TT], F32)
    sig = [sb.t([128, TT], F32) for _ in range(2)]
    glb = [sb.t([128, TT], BF16) for _ in range(2)]
    zpad = sb.t([128, KC, 16], BF16)
    r_hb = [P.res(KC) for _ in range(2)]
    P.dma("sp", C.dh[0], hb[0][:, :, :], hv[:, :, 0:TT], writes=r_hb[0])
    r_pw1 = P.res(KC)
    load_weight(P, C, pw1s, W["pw1b"].rearrange("(k p) f -> p k f", p=128), KC, r_pw1, per=2)
    r_z = P.res()
    P.op("dve", lambda e: e.memset(zpad[:, :, :], 0.0), writes=[r_z])
    P.dma("sp", C.dm[0], GLv[:, :, 0:16], zpad[:, :, :], reads=[r_z])
    P.dma("sp", C.dm[1], GLv[:, :, 16 + S:32 + S], zpad[:, :, :], reads=[r_z])
    r_hn_l = [P.res(KC) for _ in range(2)]
    r_sq = P.res(8)
    r_st, r_rstd = P.res(), P.res()
    r_psA, r_psG, r_sig, r_glb = P.res(2), P.res(2), P.res(2), P.res(2)
    psA, psG = C.ps[0:2], C.ps[2:4]
    b1 = C.cols["conv_b_pw1"]
    def front(i):
        emit_stats(P, C, hb[i % 2], r_hb[i % 2], sq, r_sq, r_st)
        emit_norm(P, C, hb[i % 2], r_hb[i % 2], gcol, rstd, r_rstd, r_st, hn_l[i % 2], r_hn_l[i % 2])

    front(0)
    for i in range(NT):
        b = i % 2
        hn, r_hn = hn_l[b], r_hn_l[b]
        if i + 1 < NT:
            P.dma("sp", C.dh[1 - b], hb[1 - b][:, :, :], hv[:, :, (i + 1) * TT:(i + 2) * TT], writes=r_hb[1 - b])
        for c in range(KC):
            if c == 3 and i + 1 < NT:
                front(i + 1)
            s = c % 2
            for k in range(KC):
                P.op("pe", lambda e: e.matmul(psA[s][:, :], lhsT=pw1s[:, k, c * 128:(c + 1) * 128], rhs=hn[:, k, :],
                                              start=(k == 0), stop=(k == KC - 1)),
                     reads=[r_pw1[k], r_hn[k]], writes=[r_psA[s]], signal=(k == KC - 1))
            for k in range(KC):
                P.op("pe", lambda e: e.matmul(psG[s][:, :], lhsT=pw1s[:, k, D + c * 128:D + (c + 1) * 128], rhs=hn[:, k, :],
                                              start=(k == 0), stop=(k == KC - 1)),
                     reads=[r_pw1[k], r_hn[k]], writes=[r_psG[s]], signal=(k == KC - 1))
            P.op("act", lambda e: e.activation(out=sig[s][:, :], in_=psG[s][:, :], func=AF.Sigmoid, bias=b1[:, 8 + c:9 + c]),
                 reads=[r_psG[s]], writes=[r_sig[s]])
            P.op("dve", lambda e: e.scalar_tensor_tensor(out=glb[s][:, :], in0=psA[s][:, :], scalar=b1[:, c:c + 1],
                                                         in1=sig[s][:, :], op0=ALU.add, op1=ALU.mult),
                 reads=[r_psA[s], r_sig[s]], writes=[r_glb[s]])
            P.dma("sp", C.dst[s], C.GL[c * 128:(c + 1) * 128, 16 + i * TT:16 + (i + 1) * TT], glb[s][:, :], reads=[r_glb[s]])
    P.barrier()
    sb.reset()
    pw2s = sb.t([128, KC, D], BF16)
    dg = sb.t([128, KC, 31, 128], BF16)
    idb = sb.t([128, 128], BF16)
    idf = sb.t([128, 128], F32)
    hb = [sb.t([128, KC, TT], F32) for _ in range(2)]
    xb = [sb.t([128, KC, TT + 32], BF16) for _ in range(2)]
    zb = sb.t([128, KC, TT], F32)
    sqz = [sb.t([128, TT], F32) for _ in range(2)]
    yb = sb.t([128, KC, TT], BF16)
    mu = sb.t([128, TT], F32)
    var = sb.t([128, TT], F32)
    tmp = [sb.t([128, TT], F32) for _ in range(2)]
    r_pw2 = P.res(KC)
    load_weight(P, C, pw2s, W["pw2b"].rearrange("(k p) f -> p k f", p=128), KC, r_pw2, per=4)
    wdw = C.cols["conv_w_dw"]
    bdw, lng, lnb, b2 = C.cols["conv_b_dw"], C.cols["conv_ln_g"], C.cols["conv_ln_b"], C.cols["conv_b_pw2"]
    r_id, r_dg = P.res(), P.res()
    P.dma("sp", C.dm[2], idb[:, :], C.tabs["ID"], writes=[r_id])
    P.op("dve", lambda e: e.tensor_copy(out=idf[:, :], in_=idb[:, :]), reads=[r_id], writes=[r_id])
    for c in range(KC):
        for j in range(31):
            P.op("dve", lambda e: e.tensor_scalar(out=dg[:, c, j, :], in0=idf[:, :], scalar1=wdw[:, c * 31 + j:c * 31 + j + 1],
                                                  scalar2=None, op0=ALU.mult), reads=[r_id], writes=[r_dg])
    r_hb = [P.res(KC) for _ in range(2)]
    r_xb, r_sqz, r_tmp = P.res(2), P.res(2), P.res(2)
    r_zb, r_yb = P.res(KC), P.res(KC)
    r_s1, r_s2, r_mu, r_var = P.res(), P.res(), P.res(), P.res()
    r_psO, r_psC = P.res(2), P.res(2)
    psS1, psS2, psO, psC = C.ps[0], C.ps[1], C.ps[2:4], C.ps[4:6]
    GLv3 = C.GL.rearrange("(k p) s -> p k s", p=128)

    def loadt(i):
        b = i % 2
        P.dma("sp", C.dh[b], hb[b][:, :, :], hv[:, :, i * TT:(i + 1) * TT], writes=r_hb[b])
        P.dma("sp", C.dm[b], xb[b][:, :, 0:TT + 30], GLv3[:, :, i * TT + 1:i * TT + 1 + TT + 30], writes=[r_xb[b]])

    loadt(0)
    for i in range(NT):
        b = i % 2
        if i + 1 < NT:
            loadt(i + 1)
        for c in range(KC):
            s = c % 2
            for j in range(31):
                P.op("pe", lambda e: e.matmul(psC[s][:, :], lhsT=dg[:, c, j, :], rhs=xb[b][:, c, j:j + TT], start=(j == 0), stop=(j == 30)),
                     reads=[r_dg, r_xb[b]], writes=[r_psC[s]], signal=(j == 30))
            P.op("dve", lambda e: e.tensor_scalar(out=zb[:, c, :], in0=psC[s][:, :], scalar1=bdw[:, c:c + 1], scalar2=None, op0=ALU.add),
                 reads=[r_psC[s]], writes=[r_zb[c]])
            P.op("act", lambda e: e.activation(out=sqz[s][:, :], in_=zb[:, c, :], func=AF.Square),
                 reads=[r_zb[c]], writes=[r_sqz[s]])
            for cp in ([c - 1] if c >= 1 else []) + ([c] if c == KC - 1 else []):
                sp_ = cp % 2
                P.op("pe", lambda e: e.matmul(psS1[:, :], lhsT=C.ones_f[:, :], rhs=zb[:, cp, :], start=(cp == 0), stop=(cp == KC - 1)),
                     reads=[r_zb[cp]], writes=[r_s1])
                P.op("pe", lambda e: e.matmul(psS2[:, :], lhsT=C.ones_f[:, :], rhs=sqz[sp_][:, :], start=(cp == 0), stop=(cp == KC - 1)),
                     reads=[r_sqz[sp_]], writes=[r_s2])
        P.op("act", lambda e: e.activation(out=mu[:, :], in_=psS1[:, :], func=AF.Copy, scale=1.0 / D), reads=[r_s1], writes=[r_mu])
        P.op("dve", lambda e: e.tensor_tensor(out=var[:, :], in0=mu[:, :], in1=mu[:, :], op=ALU.mult), reads=[r_mu], writes=[r_var])
        P.op("dve", lambda e: e.scalar_tensor_tensor(out=var[:, :], in0=psS2[:, :], scalar=1.0 / D, in1=var[:, :],
                                                     op0=ALU.mult, op1=ALU.subtract), reads=[r_s2, r_var], writes=[r_var])
        P.op("act", lambda e: e.activation(out=var[:, :], in_=var[:, :], func=AF.Sqrt, bias=C.eps_col[:, 0:1]), reads=[r_var], writes=[r_var])
        P.op("dve", lambda e: e.reciprocal(out=var[:, :], in_=var[:, :]), reads=[r_var], writes=[r_var])
        for c in range(KC):
            s = c % 2
            P.op("dve", lambda e: e.tensor_tensor(out=tmp[s][:, :], in0=zb[:, c, :], in1=mu[:, :], op=ALU.subtract),
                 reads=[r_zb[c], r_mu], writes=[r_tmp[s]])
            P.op("dve", lambda e: e.tensor_tensor(out=tmp[s][:, :], in0=tmp[s][:, :], in1=var[:, :], op=ALU.mult),
                 reads=[r_tmp[s], r_var], writes=[r_tmp[s]])
            P.op("act", lambda e: e.activation(out=yb[:, c, :], in_=tmp[s][:, :], func=AF.Silu, scale=lng[:, c:c + 1], bias=lnb[:, c:c + 1]),
                 reads=[r_tmp[s]], writes=[r_yb[c]])
        for dc in range(KC):
            s = dc % 2
            for c in range(KC):
                P.op("pe", lambda e: e.matmul(psO[s][:, :], lhsT=pw2s[:, c, dc * 128:(dc + 1) * 128], rhs=yb[:, c, :],
                                              start=(c == 0), stop=(c == KC - 1)),
                     reads=[r_pw2[c], r_yb[c]], writes=[r_psO[s]], signal=(c == KC - 1))
            P.op("dve", lambda e: e.scalar_tensor_tensor(out=hb[b][:, dc, :], in0=psO[s][:, :], scalar=b2[:, dc:dc + 1],
                                                         in1=hb[b][:, dc, :], op0=ALU.add, op1=ALU.add),
                 reads=[r_psO[s], r_hb[b][dc]], writes=[r_hb[b][dc]])
        P.dma("sp", C.dst[b], hv[:, :, i * TT:(i + 1) * TT], hb[b][:, :, :], reads=r_hb[b])
    P.barrier()


def fourier_phase(P, C, W, gcol):
    sb = C.sb
    hv = C.hs.rearrange("(k p) s -> p k s", p=128)
    sb.reset()
    hb = [sb.t([128, KC, TT], F32) for _ in range(2)]
    hn = [sb.t([128, KC, TT], BF16) for _ in range(2)]
    sq = [sb.t([128, TT], BF16)[:, :] for _ in range(8)]
    rstd = sb.t([128, TT], F32)
    r_hb = [P.res(KC) for _ in range(2)]
    r_hn = [P.res(KC) for _ in range(2)]
    r_sq = P.res(8)
    r_st, r_rstd = P.res(), P.res()
    UTv = C.UT.rearrange("(k p) s -> p k s", p=128)
    P.dma("sp", C.dh[0], hb[0][:, :, :], hv[:, :, 0:TT], writes=r_hb[0])
    for i in range(NT):
        b = i % 2
        if i + 1 < NT:
            P.dma("sp", C.dh[1 - b], hb[1 - b][:, :, :], hv[:, :, (i + 1) * TT:(i + 2) * TT], writes=r_hb[1 - b])
        emit_stats(P, C, hb[b], r_hb[b], sq, r_sq, r_st)
        emit_norm(P, C, hb[b], r_hb[b], gcol, rstd, r_rstd, r_st, hn[b], r_hn[b])
        P.dma("sp", C.dst[b], UTv[:, :, i * TT:(i + 1) * TT], hn[b][:, :, :], reads=r_hn[b])
    P.barrier()
    sb.reset()
    F1 = sb.t([64, 192], BF16)
    Mt = sb.t([128, 64, 2, 128], BF16)
    uS = [sb.t([64, 128, 128], BF16) for _ in range(2)]
    Z = sb.t([128, 128, 192], BF16)
    PTs = sb.t([128, S], BF16)
    QTs = sb.t([128, S], BF16)
    r_tab = P.res()
    P.dma("pool", C.dm[0], F1[:, :], C.tabs["F1"], writes=[r_tab])
    P.dma("pool", C.dm[1], Mt[:, :, :, :], C.tabs["Mt"], writes=[r_tab])
    r_uS = P.res(2)
    r_Z = P.res(64)
    r_psZ = P.res(2)
    r_psP, r_psQ = P.res(2), P.res(2)
    r_PT, r_QT = P.res(), P.res()
    psZ, psP, psQ = C.ps[0:2], C.ps[2:4], C.ps[4:6]
    UTa = C.UT.rearrange("ch (a b) -> a ch b", b=128)
    PTv = PTs[:, :].rearrange("p (d c) -> p c d", c=64)
    QTv = QTs[:, :].rearrange("p (d c) -> p c d", c=64)

    def loadu(cc):
        for hlf in range(2):
            P.dma("sp", C.dh[hlf], uS[cc % 2][:, hlf * 64:(hlf + 1) * 64, :],
                  UTa[:, cc * 128 + hlf * 64:cc * 128 + (hlf + 1) * 64, :], writes=[r_uS[cc % 2]])

    loadu(0)
    for cc in range(KC):
        u = uS[cc % 2]
        if cc + 1 < KC:
            loadu(cc + 1)
        for pr in range(64):
            s = pr % 2
            for q in range(2):
                ch = 2 * pr + q
                P.op("pe", lambda e: e.matmul(psZ[s][:, q * 192:(q + 1) * 192], lhsT=u[:, ch, :], rhs=F1[:, :], start=True, stop=True),
                     reads=[r_uS[cc % 2], r_tab], writes=[r_psZ[s]], signal=(q == 1))
            eng = "act" if pr % 2 == 0 else "dve"
            if eng == "act":
                P.op("act", lambda e: e.copy(out=Z[:, 2 * pr:2 * pr + 2, :], in_=psZ[s][:, 0:384].rearrange("p (q f) -> p q f", q=2)),
                     reads=[r_psZ[s]], writes=[r_Z[pr]])
            else:
                P.op("dve", lambda e: e.tensor_copy(out=Z[:, 2 * pr:2 * pr + 2, :], in_=psZ[s][:, 0:384].rearrange("p (q f) -> p q f", q=2)),
                     reads=[r_psZ[s]], writes=[r_Z[pr]])
        for c4 in range(16):
            s = c4 % 2
            for q in range(4):
                c = 4 * c4 + q
                P.op("pe", lambda e: e.matmul(psP[s][:, q * 128:(q + 1) * 128], lhsT=Z[:, :, c], rhs=Mt[:, c, 0, :], start=True, stop=False),
                     reads=r_Z + [r_tab], writes=[r_psP[s]], signal=False)
                P.op("pe", lambda e: e.matmul(psP[s][:, q * 128:(q + 1) * 128], lhsT=Z[:, :, 128 + c], rhs=Mt[:, c, 1, :], start=False, stop=True),
                     reads=r_Z + [r_tab], writes=[r_psP[s]], signal=(q == 3))
            for q in range(4):
                c = 4 * c4 + q
                P.op("pe", lambda e: e.matmul(psQ[s][:, q * 128:(q + 1) * 128], lhsT=Z[:, :, c], rhs=Mt[:, c, 1, :], start=True, stop=False),
                     reads=r_Z + [r_tab], writes=[r_psQ[s]], signal=False)
                P.op("pe", lambda e: e.matmul(psQ[s][:, q * 128:(q + 1) * 128], lhsT=Z[:, :, 64 + c], rhs=Mt[:, c, 0, :], start=False, stop=True),
                     reads=r_Z + [r_tab], writes=[r_psQ[s]], signal=(q == 3))
            P.op("act", lambda e: e.copy(out=PTv[:, 4 * c4:4 * c4 + 4, :], in_=psP[s][:, :].rearrange("p (q d) -> p q d", q=4)),
                 reads=[r_psP[s]], writes=[r_PT])
            P.op("dve", lambda e: e.tensor_copy(out=QTv[:, 4 * c4:4 * c4 + 4, :], in_=psQ[s][:, :].rearrange("p (q d) -> p q d", q=4)),
                 reads=[r_psQ[s]], writes=[r_QT])
        P.dma("sp", C.dst[0], C.PT[cc * 128:(cc + 1) * 128, :], PTs[:, :], reads=[r_PT])
        P.dma("sp", C.dst[1], C.QT[cc * 128:(cc + 1) * 128, :], QTs[:, :], reads=[r_QT])
    P.barrier()
    sb.reset()
    wfs = sb.t([128, KC, D], BF16)
    CS = sb.t([128, 2, 2, 256], BF16)
    hb = [sb.t([128, KC, TT], F32) for _ in range(2)]
    pq = [[sb.t([128, KC, TT], BF16) for _ in range(2)] for _ in range(2)]
    mx = sb.t([128, KC, TT], BF16)
    r_wf = P.res(KC)
    load_weight(P, C, wfs, W["wfb"].rearrange("(k p) f -> p k f", p=128), KC, r_wf, per=4)
    r_cs = P.res()
    P.dma("pool", C.dm[0], CS[:, :, :, :], C.tabs["CS"], writes=[r_cs])
    r_hb = [P.res(KC) for _ in range(2)]
    r_p = [P.res() for _ in range(2)]
    r_q = [P.res() for _ in range(2)]
    r_mx = P.res(KC)
    r_psM, r_psO = P.res(2), P.res(2)
    psM, psO = C.ps[0:2], C.ps[2:4]
    PTd = C.PT.rearrange("(k p) s -> p k s", p=128)
    QTd = C.QT.rearrange("(k p) s -> p k s", p=128)
    bfc = C.cols["fnet_b"]
    scl = 1.0 / math.sqrt(S * 256.0)

    def loadt(i):
        b = i % 2
        P.dma("sp", C.dh[b], hb[b][:, :, :], hv[:, :, i * TT:(i + 1) * TT], writes=r_hb[b])
        P.dma("sp", C.dm[b], pq[b][0][:, :, :], PTd[:, :, i * TT:(i + 1) * TT], writes=[r_p[b]])
        P.dma("sp", C.dm[2 + b], pq[b][1][:, :, :], QTd[:, :, i * TT:(i + 1) * TT], writes=[r_q[b]])

    loadt(0)
    for i in range(NT):
        b = i % 2
        if i + 1 < NT:
            loadt(i + 1)
        for g in range(4):
            for j in range(2):
                oc = 2 * g + j
                s = oc % 2
                for n in range(2):
                    P.op("pe", lambda e: e.matmul(psM[s][:, :], lhsT=CS[:, n, 0, j * 128:(j + 1) * 128], rhs=pq[b][0][:, 2 * g + n, :],
                                                  start=(n == 0), stop=False),
                         reads=[r_cs, r_p[b]], writes=[r_psM[s]], signal=False)
                    P.op("pe", lambda e: e.matmul(psM[s][:, :], lhsT=CS[:, n, 1, j * 128:(j + 1) * 128], rhs=pq[b][1][:, 2 * g + n, :],
                                                  start=False, stop=(n == 1)),
                         reads=[r_cs, r_q[b]], writes=[r_psM[s]], signal=(n == 1))
                P.op("act", lambda e: e.activation(out=mx[:, oc, :], in_=psM[s][:, :], func=AF.Copy, scale=scl),
                     reads=[r_psM[s]], writes=[r_mx[oc]])
        for dc in range(KC):
            s = dc % 2
            for k in range(KC):
                P.op("pe", lambda e: e.matmul(psO[s][:, :], lhsT=wfs[:, k, dc * 128:(dc + 1) * 128], rhs=mx[:, k, :],
                                              start=(k == 0), stop=(k == KC - 1)),
                     reads=[r_wf[k], r_mx[k]], writes=[r_psO[s]], signal=(k == KC - 1))
            P.op("dve", lambda e: e.scalar_tensor_tensor(out=hb[b][:, dc, :], in0=psO[s][:, :], scalar=bfc[:, dc:dc + 1],
                                                         in1=hb[b][:, dc, :], op0=ALU.add, op1=ALU.add),
                 reads=[r_psO[s], r_hb[b][dc]], writes=[r_hb[b][dc]])
        P.dma("sp", C.dst[b], hv[:, :, i * TT:(i + 1) * TT], hb[b][:, :, :], reads=r_hb[b])
    P.barrier()


PADK = 1024
LS = 2048


def attn_phase(P, C, W, gcol):
    sb = C.sb
    hv = C.hs.rearrange("(k p) s -> p k s", p=128)
    sb.reset()
    wq = sb.t([128, KC, 4608], BF16)
    hb = [sb.t([128, KC, TT], F32) for _ in range(2)]
    hn_l = [sb.t([128, KC, TT], BF16) for _ in range(2)]
    sq = [sb.t([128, TT], BF16)[:, :] for _ in range(8)]
    rstd = sb.t([128, TT], F32)
    qk = [sb.t([128, TT], BF16) for _ in range(4)]
    va = [sb.t([128, 8, 65], BF16) for _ in range(4)]
    zt = sb.t([128, 8 * 520], BF16)
    r_hb = [P.res(KC) for _ in range(2)]
    P.dma("sp", C.dh[0], hb[0][:, :, :], hv[:, :, 0:TT], writes=r_hb[0])
    r_wq = P.res(KC)
    load_weight(P, C, wq, W["wqkvb"].rearrange("(k p) f -> p k f", p=128), KC, r_wq, per=1)
    r_z = P.res()
    r_va = P.res(4)
    P.op("pool", lambda e: e.memset(zt[:, :], 0.0), writes=[r_z])
    for q in range(4):
        P.op("pool", lambda e: e.memset(va[q][:, :, :], 1.0), writes=[r_va[q]])
    for g in range(3):
        for side in range(2):
            c0 = 0 if side == 0 else PADK + S
            P.dma("sp", C.dm[side], C.KT[g].rearrange("(c p) s -> p c s", p=128)[:, :, c0:c0 + PADK],
                  zt[:, 0:4 * PADK].rearrange("p (c s) -> p c s", c=4), reads=[r_z])
            P.dma("sp", C.dm[2 + side], C.VA[g][c0:c0 + PADK, :].rearrange("(p k) f -> p (k f)", p=128), zt[:, :], reads=[r_z])
    r_hn_l = [P.res(KC) for _ in range(2)]
    r_sq = P.res(8)
    r_st, r_rstd = P.res(), P.res()
    r_ps = P.res(4)
    r_qk = P.res(4)
    ps = C.ps[0:4]
    dq = C.dst + C.dm[0:2]
    n = 0

    def front(i):
        emit_stats(P, C, hb[i % 2], r_hb[i % 2], sq, r_sq, r_st)
        emit_norm(P, C, hb[i % 2], r_hb[i % 2], gcol, rstd, r_rstd, r_st, hn_l[i % 2], r_hn_l[i % 2])

    front(0)
    for i in range(NT):
        b = i % 2
        hn, r_hn = hn_l[b], r_hn_l[b]
        if i + 1 < NT:
            P.dma("sp", C.dh[1 - b], hb[1 - b][:, :, :], hv[:, :, (i + 1) * TT:(i + 2) * TT], writes=r_hb[1 - b])
        for g in range(3):
            if g == 1 and i + 1 < NT:
                front(i + 1)
            for t in range(2):
                for c in range(4):
                    s = n % 4
                    col = g * 1536 + t * 512 + c * 128
                    for k in range(KC):
                        P.op("pe", lambda e: e.matmul(ps[s][:, :], lhsT=wq[:, k, col:col + 128], rhs=hn[:, k, :],
                                                      start=(k == 0), stop=(k == KC - 1)),
                             reads=[r_wq[k], r_hn[k]], writes=[r_ps[s]], signal=(k == KC - 1))
                    if n % 2 == 0:
                        P.op("act", lambda e: e.copy(out=qk[s][:, :], in_=ps[s][:, :]), reads=[r_ps[s]], writes=[r_qk[s]])
                    else:
                        P.op("dve", lambda e: e.tensor_copy(out=qk[s][:, :], in_=ps[s][:, :]), reads=[r_ps[s]], writes=[r_qk[s]])
                    if t == 0:
                        dst = C.AQ[g][c * 128:(c + 1) * 128, i * TT:(i + 1) * TT]
                    else:
                        dst = C.KT[g][c * 128:(c + 1) * 128, PADK + i * TT:PADK + (i + 1) * TT]
                    P.dma("sp", dq[s], dst, qk[s][:, :], reads=[r_qk[s]])
                    n += 1
            for j in range(4):
                s = n % 4
                col = g * 1536 + 1024
                for k in range(KC):
                    P.op("pe", lambda e: e.matmul(ps[s][:, :], lhsT=hn[:, k, j * 128:(j + 1) * 128], rhs=wq[:, k, col:col + 512],
                                                  start=(k == 0), stop=(k == KC - 1)),
                         reads=[r_wq[k], r_hn[k]], writes=[r_ps[s]], signal=(k == KC - 1))
                src = ps[s][:, :].rearrange("p (h e) -> p h e", h=8)
                if n % 2 == 0:
                    P.op("act", lambda e: e.copy(out=va[s][:, :, 0:64], in_=src), reads=[r_ps[s]], writes=[r_va[s]])
                else:
                    P.op("dve", lambda e: e.tensor_copy(out=va[s][:, :, 0:64], in_=src), reads=[r_ps[s]], writes=[r_va[s]])
                r0 = PADK + i * TT + j * 128
                P.dma("sp", dq[s], C.VA[g][r0:r0 + 128, :], va[s][:, :, :].rearrange("p h e -> p (h e)"), reads=[r_va[s]])
                n += 1
    P.barrier()
    if os.environ.get("ATT_STOP") == "1":
        return
    sb.reset()
    ET = sb.t([128, 3, 2, 8, 128], F32)
    Qs = [sb.t([64, 4, LS], BF16) for _ in range(2)]
    Ks = [sb.t([64, 4, LS + 2048], BF16) for _ in range(2)]
    Vs = [sb.t([128, 32 * 260 + 260], BF16) for _ in range(2)]
    eb = [[sb.t([128, 512], F32) for _ in range(2)] for _ in range(2)]
    pT = [[sb.t([128, 4, 128], BF16) for _ in range(2)] for _ in range(2)]
    nz = [sb.t([128, 260], F32) for _ in range(2)]
    r_et = P.res()
    P.dma("sp", C.dm[0], ET[:, :, :, :, :].rearrange("p a b c d -> p (a b c d)"), C.tabs["EA"], writes=[r_et])
    r_Q, r_K, r_V = P.res(2), P.res(2), P.res(2)
    r_psS, r_eb, r_pT = [P.res(2) for _ in range(2)], [P.res(2) for _ in range(2)], [P.res(2) for _ in range(2)]
    r_psO, r_nz = P.res(2), P.res(2)
    psS, psO = [C.ps[0:2], C.ps[2:4]], C.ps[4:6]
    sets = [(g, sp, hq) for g in range(3) for sp in range(S // LS) for hq in range(2)]
    if os.environ.get("ATT_SETS"):
        lo_, hi_ = os.environ["ATT_SETS"].split(":")
        sets = sets[int(lo_):int(hi_)]

    def loadset(idx):
        g, sp, hq = sets[idx]
        dil = DIL[g]
        b = idx % 2
        t0 = sp * LS
        NB = LS // (128 * dil)
        QTv = C.AQ[g].rearrange("(h e) s -> e h s", e=64)
        KTv = C.KT[g].rearrange("(h e) s -> e h s", e=64)
        P.dma("sp", C.dh[b], Qs[b][:, :, :], QTv[:, 4 * hq:4 * hq + 4, t0:t0 + LS], writes=[r_Q[b]])
        kw = LS + 128 * dil
        k0 = PADK + t0 - 64 * dil
        P.dma("sp", C.dm[b], Ks[b][:, :, 0:kw], KTv[:, 4 * hq:4 * hq + 4, k0:k0 + kw], writes=[r_K[b]])
        vsrc = C.VA[g][k0:k0 + (NB + 1) * 128 * dil, :].rearrange("(c k d) f -> k c d f", k=128, d=dil)[:, :, :, hq * 260:(hq + 1) * 260]
        vdst = Vs[b][:, 0:(NB + 1) * dil * 260].rearrange("p (c d f) -> p c d f", c=NB + 1, d=dil)
        if dil == 1:
            P.dma("sp", C.dm[2 + b], vdst[:, :, 0, :], vsrc[:, :, 0, :], writes=[r_V[b]])
        else:
            for cch in range(NB + 1):
                P.dma("sp", C.dm[2 + b], vdst[:, cch, :, :], vsrc[:, cch, :, :], writes=[r_V[b]], nowait=(cch > 0))

    loadset(0)
    un = 0
    for idx in range(len(sets)):
        g, sp, hq = sets[idx]
        dil = DIL[g]
        b = idx % 2
        t0 = sp * LS
        NB = LS // (128 * dil)
        if idx + 1 < len(sets):
            loadset(idx + 1)
        Vv = Vs[b][:, 0:(NB + 1) * dil * 260].rearrange("p (c d f) -> p c d f", c=NB + 1, d=dil)

        def front(nb, r, u2):
            for AB in range(2):
                koff = r + dil * (128 * nb + AB * 128)
                qoff = r + dil * 128 * nb
                for hh in range(4):
                    P.op("pe", lambda e: e.matmul(psS[u2][AB][:, hh * 128:(hh + 1) * 128],
                                                  lhsT=Ks[b][:, hh, koff:koff + 127 * dil + 1:dil],
                                                  rhs=Qs[b][:, hh, qoff:qoff + 127 * dil + 1:dil], start=True, stop=True),
                         reads=[r_K[b], r_Q[b]], writes=[r_psS[u2][AB]], signal=(hh == 3))
                P.op("act", lambda e: e.activation(out=eb[u2][AB][:, :], in_=psS[u2][AB][:, :], func=AF.Exp, scale=0.125),
                     reads=[r_psS[u2][AB]], writes=[r_eb[u2][AB]])
                P.op("dve", lambda e: e.tensor_tensor(out=pT[u2][AB][:, :, :].rearrange("p h q -> p (h q)"), in0=eb[u2][AB][:, :],
                                                      in1=ET[:, g, AB, 4 * hq:4 * hq + 4, :].rearrange("p h q -> p (h q)"), op=ALU.mult),
                     reads=[r_eb[u2][AB], r_et], writes=[r_pT[u2][AB]])

        def back(nb, r, u2, uidx):
            for hh in range(4):
                for AB in range(2):
                    P.op("pe", lambda e: e.matmul(psO[u2][:, hh * 65:(hh + 1) * 65], lhsT=pT[u2][AB][:, hh, :],
                                                  rhs=Vv[:, nb + AB, r, hh * 65:(hh + 1) * 65], start=(AB == 0), stop=(AB == 1)),
                         reads=[r_pT[u2][AB], r_V[b]], writes=[r_psO[u2]], signal=(hh == 3 and AB == 1))
            if uidx % 2 == 0:
                P.op("act", lambda e: e.copy(out=nz[u2][:, :], in_=psO[u2][:, 0:260]), reads=[r_psO[u2]], writes=[r_nz[u2]])
            else:
                P.op("dve", lambda e: e.tensor_copy(out=nz[u2][:, :], in_=psO[u2][:, 0:260]), reads=[r_psO[u2]], writes=[r_nz[u2]])
            tq0 = t0 + dil * 128 * nb
            dstv = C.NZ[g][tq0:tq0 + 128 * dil, :].rearrange("(q d) f -> q d f", d=dil)[:, r, hq * 260:(hq + 1) * 260]
            P.dma("sp", C.dst[u2], dstv, nz[u2][:, :], reads=[r_nz[u2]])

        pend = None
        for nb in range(NB):
            for r in range(dil):
                u2 = un % 2
                front(nb, r, u2)
                if pend is not None:
                    back(*pend)
                pend = (nb, r, u2, un)
                un += 1
        back(*pend)
    P.barrier()
    if os.environ.get("ATT_STOP") == "2":
        return
    sb.reset()
    wos = sb.t([128, 4, D], BF16)
    ident = sb.t([128, 128], BF16)
    hb = [sb.t([128, KC, TT], F32) for _ in range(2)]
    nzl = [[sb.t([128, 4, 520], F32) for _ in range(3)] for _ in range(2)]
    nsum = [sb.t([128, 8, 65], F32) for _ in range(2)]
    rz = [sb.t([128, 8, 1], F32) for _ in range(2)]
    ob = [sb.t([128, 8, 64], BF16) for _ in range(2)]
    oT = sb.t([128, 4, TT], BF16)
    r_wo = P.res(4)
    load_weight(P, C, wos, W["wob"].rearrange("(k p) f -> p k f", p=128), 4, r_wo, per=4)
    r_id = P.res()
    P.dma("sp", C.dm[0], ident[:, :], C.tabs["ID"], writes=[r_id])
    r_hb = [P.res(KC) for _ in range(2)]
    r_nzl = [P.res(3) for _ in range(2)]
    r_ns, r_rz, r_ob = P.res(2), P.res(2), P.res(2)
    r_psT = P.res(2)
    r_oT = P.res(4)
    r_psO = P.res(2)
    psT, psO = C.ps[2:4], C.ps[0:2]
    dl = [C.dm[0:2], C.dm[2:4], C.dw[0:2]]

    def loadt(i):
        b = i % 2
        P.dma("sp", C.dh[b], hb[b][:, :, :], hv[:, :, i * TT:(i + 1) * TT], writes=r_hb[b])
        for g in range(3):
            P.dma("sp", dl[g][b], nzl[b][g][:, :, :], C.NZ[g][i * TT:(i + 1) * TT, :].rearrange("(j p) f -> p j f", p=128),
                  writes=[r_nzl[b][g]])

    loadt(0)
    for i in range(NT):
        b = i % 2
        if i + 1 < NT:
            loadt(i + 1)
        for j in range(4):
            s = j % 2
            nsf = nsum[s][:, :, :].rearrange("p h e -> p (h e)")
            P.op("pool", lambda e: e.tensor_tensor(out=nsf, in0=nzl[b][0][:, j, :], in1=nzl[b][1][:, j, :], op=ALU.add),
                 reads=[r_nzl[b][0], r_nzl[b][1]], writes=[r_ns[s]])
            P.op("pool", lambda e: e.tensor_tensor(out=nsf, in0=nsf, in1=nzl[b][2][:, j, :], op=ALU.add),
                 reads=[r_nzl[b][2], r_ns[s]], writes=[r_ns[s]])
            P.op("dve", lambda e: e.reciprocal(out=rz[s][:, :, :], in_=nsum[s][:, :, 64:65]), reads=[r_ns[s]], writes=[r_rz[s]])
            P.op("dve", lambda e: e.tensor_tensor(out=ob[s][:, :, :], in0=nsum[s][:, :, 0:64], in1=rz[s][:, :, :].to_broadcast([128, 8, 64]),
                                                  op=ALU.mult), reads=[r_ns[s], r_rz[s]], writes=[r_ob[s]])
            for fc in range(4):
                P.op("pe", lambda e: e.matmul(psT[s][:, fc * 128:(fc + 1) * 128],
                                              lhsT=ob[s][:, 2 * fc:2 * fc + 2, :].rearrange("p h e -> p (h e)"), rhs=ident[:, :],
                                              start=True, stop=True),
                     reads=[r_ob[s], r_id], writes=[r_psT[s]], signal=(fc == 3))
            P.op("act", lambda e: e.copy(out=oT[:, :, j * 128:(j + 1) * 128], in_=psT[s][:, :].rearrange("p (c q) -> p c q", c=4)),
                 reads=[r_psT[s]], writes=[r_oT[j]])
        for dc in range(KC):
            s = dc % 2
            for fc in range(4):
                P.op("pe", lambda e: e.matmul(psO[s][:, :], lhsT=wos[:, fc, dc * 128:(dc + 1) * 128], rhs=oT[:, fc, :],
                                              start=(fc == 0), stop=(fc == 3)),
                     reads=[r_wo[fc]] + r_oT, writes=[r_psO[s]], signal=(fc == 3))
            P.op("dve", lambda e: e.tensor_tensor(out=hb[b][:, dc, :], in0=psO[s][:, :], in1=hb[b][:, dc, :], op=ALU.add),
                 reads=[r_psO[s], r_hb[b][dc]], writes=[r_hb[b][dc]])
        P.dma("sp", C.dst[b], hv[:, :, i * TT:(i + 1) * TT], hb[b][:, :, :], reads=r_hb[b])
    P.barrier()


def gla_phase(P, C, W, gcol):
    sb = C.sb
    hv = C.hs.rearrange("(k p) s -> p k s", p=128)
    NCH = S // 128
    for pas in range(int(os.environ.get("GLA_PASSES", "2"))):
        dirn = 1 - pas
        last = (pas == 1)
        sb.reset()
        win = sb.t([128, KC, 3072], BF16)
        wa1 = sb.t([128, KC, 16], BF16)
        wa2a = sb.t([17, 512], BF16)
        GLt = sb.t([128, 4, 128], F32)
        GMt = sb.t([128, 2, 4, 128], F32)
        hb = [sb.t([128, KC, TT], F32) for _ in range(2)]
        hn_l = [sb.t([128, KC, TT], BF16) for _ in range(2)]
        sq = [sb.t([128, TT], BF16)[:, :] for _ in range(4)]
        rstd = sb.t([128, TT], F32)
        qT = sb.t([128, 4, TT], F32)
        kT = sb.t([128, 4, TT], F32)
        t1a = sb.t([32, TT], BF16)
        Sm = sb.t([128, 4, 256], F32)
        Sb = sb.t([128, 4, 256], BF16)
        lt = sb.t([128, 512], F32)
        lg = sb.t([128, 512], F32)
        Eq = sb.t([128, 512], F32)
        Ek = sb.t([128, 512], F32)
        eks = sb.t([128, 512], F32)
        qtl = sb.t([128, 4, 128], BF16)
        ktl = sb.t([128, 4, 128], BF16)
        ks = sb.t([128, 512], BF16)
        vt = sb.t([128, 1024], BF16)
        sT = sb.t([128, 4, 128], BF16)
        osb = [sb.t([128, 4, 256], F32) for _ in range(2)]
        if last:
            wos = sb.t([128, KC, D], BF16)
            ident = sb.t([128, 128], BF16)
            ngb = sb.t([128, 1024], F32)
            obl = [sb.t([128, 1024], F32) for _ in range(2)]
            sr = sb.t([128, 1024], F32)
            junk = sb.t([128, 4, 256], F32)
            ssq = sb.t([128, 4, 1], F32)
            yb = sb.t([128, 1024], BF16)
            yT = sb.t([128, KC, TT], BF16)
        order = list(range(NT)) if dirn == 0 else list(range(NT - 1, -1, -1))
        r_hb = [P.res(KC) for _ in range(2)]
        P.dma("sp", C.dh[order[0] % 2], hb[order[0] % 2][:, :, :], hv[:, :, order[0] * TT:(order[0] + 1) * TT], writes=r_hb[order[0] % 2])
        r_win, r_wa1 = P.res(KC), P.res()
        load_weight(P, C, win, W["winb"].rearrange("(k p) f -> p k f", p=128), KC, r_win, per=2)
        P.dma("sp", C.dm[0], wa1[:, :, :], W["wa1b"][dirn].rearrange("(k p) f -> p k f", p=128), writes=[r_wa1], reads=[C.cast_res["wa1b"]])
        r_wa2, r_tab = P.res(), P.res()
        P.dma("pool", C.dm[1], wa2a[0:16, :], C.ext["gla_w_a2"][0, dirn], writes=[r_wa2])
        P.dma("pool", C.dm[2], wa2a[16:17, :], C.ext["gla_b_a"][0, dirn:dirn + 1, :], writes=[r_wa2])
        P.dma("sp", C.dm[3], GLt[:, :, :].rearrange("p a b -> p (a b)"), C.tabs["GL"], writes=[r_tab])
        P.dma("sp", C.dm[0], GMt[:, :, :, :].rearrange("p a b c -> p (a b c)"), C.tabs["GM"], writes=[r_tab])
        r_S, r_Sb = P.res(4), P.res(4)
        r_t1a = P.res()
        P.op("pool", lambda e: e.memset(Sm[:, :, :], 0.0), writes=r_S)
        P.op("pool", lambda e: e.memset(Sb[:, :, :], 0.0), writes=r_Sb)
        P.op("pool", lambda e: e.memset(t1a[:, :], 1.0), writes=[r_t1a])
        if last:
            r_wo, r_id, r_ng = P.res(KC), P.res(), P.res()
            load_weight(P, C, wos, W["gwob"].rearrange("(k p) f -> p k f", p=128), KC, r_wo, per=4)
            P.dma("sp", C.dm[1], ident[:, :], C.tabs["ID"], writes=[r_id])
            P.dma("sp", C.dm[2], ngb[:, :], C.ngd, writes=[r_ng])
            r_obl, r_sr, r_junk, r_ssq, r_yb = P.res(2), P.res(), P.res(), P.res(), P.res()
            r_yT = P.res(4)
            r_psT = P.res(2)
        r_hn_l = [P.res(KC) for _ in range(2)]
        r_sq = P.res(4)
        r_st, r_rstd = P.res(), P.res()
        r_qT, r_kT = P.res(), P.res()
        r_lt, r_lg, r_Eq, r_Ek, r_eks, r_qtl, r_ktl, r_ks, r_vt, r_sT = (P.res() for _ in range(10))
        r_osb = P.res(2)
        r_bank = P.res(6)
        bk = [0]

        def bank():
            i = bk[0] % 6
            bk[0] += 1
            return C.ps[i], r_bank[i]

        def loadh(i):
            b = i % 2
            P.dma("sp", C.dh[b], hb[b][:, :, :], hv[:, :, i * TT:(i + 1) * TT], writes=r_hb[b])

        pendT = []

        def flushT():
            while pendT:
                jj = pendT.pop(0)
                tj = slice(jj * 128, (jj + 1) * 128)
                for half in range(2):
                    pt_, rt_ = bank()
                    for q in range(4):
                        fc = half * 4 + q
                        P.op("pe", lambda e: e.matmul(pt_[:, q * 128:(q + 1) * 128], lhsT=yb[:, fc * 128:(fc + 1) * 128], rhs=ident[:, :],
                                                      start=True, stop=True),
                             reads=[r_yb, r_id], writes=[rt_], signal=(q == 3))
                    src = pt_[:, :].rearrange("p (c q) -> p c q", c=4)
                    if half == 0:
                        P.op("act", lambda e: e.copy(out=yT[:, 0:4, tj], in_=src), reads=[rt_], writes=[r_yT[jj]])
                    else:
                        P.op("dve", lambda e: e.tensor_copy(out=yT[:, 4:8, tj], in_=src), reads=[rt_], writes=[r_yT[jj]])

        un = 0
        for oi, i in enumerate(order):
            b = i % 2
            if oi + 1 < NT:
                loadh(order[oi + 1])
            hn, r_hn = hn_l[oi % 2], r_hn_l[oi % 2]
            if oi == 0:
                emit_stats(P, C, hb[b], r_hb[b], sq, r_sq, r_st)
                emit_norm(P, C, hb[b], r_hb[b], gcol, rstd, r_rstd, r_st, hn, r_hn)
            for t in range(2):
                dstT, r_dst = (qT, r_qT) if t == 0 else (kT, r_kT)
                for hd in range(4):
                    pb, rb = bank()
                    col = t * 512 + hd * 128
                    for k in range(KC):
                        P.op("pe", lambda e: e.matmul(pb[:, :], lhsT=win[:, k, col:col + 128], rhs=hn[:, k, :], start=(k == 0), stop=(k == KC - 1)),
                             reads=[r_win[k], r_hn[k]], writes=[rb], signal=(k == KC - 1))
                    if hd % 2 == 0:
                        P.op("act", lambda e: e.copy(out=dstT[:, hd, :], in_=pb[:, :]), reads=[rb], writes=[r_dst])
                    else:
                        P.op("dve", lambda e: e.tensor_copy(out=dstT[:, hd, :], in_=pb[:, :]), reads=[rb], writes=[r_dst])
            pb, rb = bank()
            for k in range(KC):
                P.op("pe", lambda e: e.matmul(pb[0:16, :], lhsT=wa1[:, k, :], rhs=hn[:, k, :], start=(k == 0), stop=(k == KC - 1)),
                     reads=[r_wa1, r_hn[k]], writes=[rb], signal=(k == KC - 1))
            P.op("act", lambda e: e.copy(out=t1a[0:16, :], in_=pb[0:16, :]), reads=[rb], writes=[r_t1a])
            jl = list(range(4)) if dirn == 0 else [3, 2, 1, 0]
            for jn, j in enumerate(jl):
                if jn == 2 and oi + 1 < NT:
                    bn = order[oi + 1] % 2
                    emit_stats(P, C, hb[bn], r_hb[bn], sq, r_sq, r_st)
                    emit_norm(P, C, hb[bn], r_hb[bn], gcol, rstd, r_rstd, r_st, hn_l[(oi + 1) % 2], r_hn_l[(oi + 1) % 2])
                ch = i * 4 + j
                tsl = slice(j * 128, (j + 1) * 128)
                if last:
                    ob_ = un % 2
                    P.dma("sp", C.dm[ob_], obl[ob_][:, :], C.OB[ch * 128:(ch + 1) * 128, :], writes=[r_obl[ob_]])
                pz, rz_ = bank()
                P.op("pe", lambda e: e.matmul(pz[:, :], lhsT=t1a[0:17, tsl], rhs=wa2a[:, :], start=True, stop=True),
                     reads=[r_t1a, r_wa2], writes=[rz_])
                P.op("act", lambda e: e.activation(out=lt[:, :], in_=pz[:, :], func=AF.Exp, scale=-1.0), reads=[rz_], writes=[r_lt])
                P.op("act", lambda e: e.activation(out=lg[:, :], in_=lt[:, :], func=AF.Ln, bias=C.one_col[:, 0:1]), reads=[r_lt], writes=[r_lg])
                pk, rk_ = bank()
                for k in range(KC):
                    P.op("pe", lambda e: e.matmul(pk[:, :], lhsT=hn[:, k, tsl], rhs=win[:, k, 512:1024], start=(k == 0), stop=(k == KC - 1)),
                         reads=[r_win[k], r_hn[k]], writes=[rk_], signal=(k == KC - 1))
                for hv_ in range(2):
                    pv, rv_ = bank()
                    for k in range(KC):
                        P.op("pe", lambda e: e.matmul(pv[:, :], lhsT=hn[:, k, tsl], rhs=win[:, k, 1024 + hv_ * 512:1536 + hv_ * 512],
                                                      start=(k == 0), stop=(k == KC - 1)),
                             reads=[r_win[k], r_hn[k]], writes=[rv_], signal=(k == KC - 1))
                    P.op("act", lambda e: e.copy(out=vt[:, hv_ * 512:(hv_ + 1) * 512], in_=pv[:, :]), reads=[rv_], writes=[r_vt])
                pc, rc_ = bank()
                for h in range(4):
                    P.op("pe", lambda e: e.matmul(pc[:, h * 128:(h + 1) * 128], lhsT=lg[:, h * 128:(h + 1) * 128], rhs=GLt[:, dirn, :],
                                                  start=True, stop=True), reads=[r_lg, r_tab], writes=[rc_], signal=(h == 3))
                pd, rd_ = bank()
                P.op("pe", lambda e: e.matmul(pd[:, :], lhsT=GLt[:, 2 + dirn, :], rhs=lg[:, :], start=True, stop=True),
                     reads=[r_lg, r_tab], writes=[rd_])
                P.op("act", lambda e: e.activation(out=Eq[:, :], in_=pc[:, :], func=AF.Exp), reads=[rc_], writes=[r_Eq])
                P.op("act", lambda e: e.activation(out=Ek[:, :], in_=pc[:, :], func=AF.Exp, scale=-1.0), reads=[rc_], writes=[r_Ek])
                P.op("act", lambda e: e.activation(out=eks[:, :], in_=pd[:, :], func=AF.Exp), reads=[rd_], writes=[r_eks])
                P.op("dve", lambda e: e.scalar_tensor_tensor(out=qtl[:, :, :], in0=qT[:, :, tsl], scalar=128.0 ** -0.5,
                                                             in1=Eq[:, :].rearrange("p (h t) -> p h t", h=4), op0=ALU.mult, op1=ALU.mult),
                     reads=[r_qT, r_Eq], writes=[r_qtl])
                P.op("dve", lambda e: e.tensor_tensor(out=ktl[:, :, :], in0=kT[:, :, tsl], in1=Ek[:, :].rearrange("p (h t) -> p h t", h=4),
                                                      op=ALU.mult), reads=[r_kT, r_Ek], writes=[r_ktl])
                P.op("dve", lambda e: e.tensor_tensor(out=ks[:, :], in0=pk[:, :], in1=eks[:, :], op=ALU.mult), reads=[rk_, r_eks], writes=[r_ks])
                if last:
                    for hv_ in range(2):
                        pr, rr_ = bank()
                        for k in range(KC):
                            P.op("pe", lambda e: e.matmul(pr[:, :], lhsT=hn[:, k, tsl], rhs=win[:, k, 2048 + hv_ * 512:2560 + hv_ * 512],
                                                          start=(k == 0), stop=(k == KC - 1)),
                                 reads=[r_win[k], r_hn[k]], writes=[rr_], signal=(k == KC - 1))
                        P.op("act", lambda e: e.activation(out=sr[:, hv_ * 512:(hv_ + 1) * 512], in_=pr[:, :], func=AF.Silu), reads=[rr_], writes=[r_sr])
                psc, rsc = bank()
                for h in range(4):
                    P.op("pe", lambda e: e.matmul(psc[:, h * 128:(h + 1) * 128], lhsT=ktl[:, h, :], rhs=qtl[:, h, :], start=True, stop=True),
                         reads=[r_ktl, r_qtl], writes=[rsc], signal=(h == 3))
                P.op("dve", lambda e: e.tensor_tensor(out=sT[:, :, :].rearrange("p h t -> p (h t)"), in0=psc[:, :],
                                                      in1=GMt[:, dirn, :, :].rearrange("p h t -> p (h t)"), op=ALU.mult),
                     reads=[rsc, r_tab], writes=[r_sT])
                po = [bank(), bank()]
                for h in range(4):
                    pb_, rb_ = po[h // 2]
                    osl = slice((h % 2) * 256, (h % 2) * 256 + 256)
                    P.op("pe", lambda e: e.matmul(pb_[:, osl], lhsT=sT[:, h, :], rhs=vt[:, h * 256:(h + 1) * 256], start=True, stop=False),
                         reads=[r_sT, r_vt], writes=[rb_], signal=False)
                    P.op("pe", lambda e: e.matmul(pb_[:, osl], lhsT=qtl[:, h, :], rhs=Sb[:, h, :], start=False, stop=True),
                         reads=[r_qtl, r_Sb[h]], writes=[rb_], signal=(h % 2 == 1))
                tl = 127 if dirn == 0 else 0
                pst = [bank(), bank()]
                for h in range(4):
                    pb_, rb_ = pst[h // 2]
                    osl = slice((h % 2) * 256, (h % 2) * 256 + 256)
                    P.op("pe", lambda e: e.matmul(pb_[:, osl], lhsT=ks[:, h * 128:(h + 1) * 128], rhs=vt[:, h * 256:(h + 1) * 256], start=True, stop=True),
                         reads=[r_ks, r_vt], writes=[rb_])
                    P.op("dve", lambda e: e.scalar_tensor_tensor(out=Sm[:, h, :], in0=Sm[:, h, :], scalar=Eq[:, h * 128 + tl:h * 128 + tl + 1],
                                                                 in1=pb_[:, osl], op0=ALU.mult, op1=ALU.add),
                         reads=[r_S[h], r_Eq, rb_], writes=[r_S[h]])
                    P.op("act", lambda e: e.copy(out=Sb[:, h, :], in_=Sm[:, h, :]), reads=[r_S[h]], writes=[r_Sb[h]])
                if last:
                    flushT()
                so = un % 2
                osf = osb[so][:, :, :].rearrange("p h v -> p (h v)")
                if not last:
                    P.op("act", lambda e: e.copy(out=osf[:, 0:512], in_=po[0][0][:, :]), reads=[po[0][1]], writes=[r_osb[so]])
                    P.op("dve", lambda e: e.tensor_copy(out=osf[:, 512:1024], in_=po[1][0][:, :]), reads=[po[1][1]], writes=[r_osb[so]])
                    P.dma("sp", C.dst[so], C.OB[ch * 128:(ch + 1) * 128, :], osf, reads=[r_osb[so]])
                else:
                    for q in range(2):
                        P.op("dve", lambda e: e.tensor_tensor(out=osf[:, q * 512:(q + 1) * 512], in0=po[q][0][:, :],
                                                              in1=obl[ob_][:, q * 512:(q + 1) * 512], op=ALU.add),
                             reads=[po[q][1], r_obl[ob_]], writes=[r_osb[so]])
                    LV2 = int(os.environ.get("GLA_L2", "9"))
                    if LV2 < 1:
                        un += 1
                        continue
                    P.op("act", lambda e: e.activation(out=junk[:, :, :], in_=osb[so][:, :, :], func=AF.Square),
                         reads=[r_osb[so]], writes=[r_junk])
                    P.op("dve", lambda e: e.reduce_sum(out=ssq[:, :, :], in_=junk[:, :, :], axis=mybir.AxisListType.X),
                         reads=[r_junk], writes=[r_ssq])
                    P.op("act", lambda e: e.activation(out=ssq[:, :, :], in_=ssq[:, :, :], func=AF.Sqrt, scale=1.0 / 256, bias=C.eps_col[:, 0:1]),
                         reads=[r_ssq], writes=[r_ssq])
                    P.op("dve", lambda e: e.reciprocal(out=ssq[:, :, :], in_=ssq[:, :, :]), reads=[r_ssq], writes=[r_ssq])
                    P.op("dve", lambda e: e.tensor_tensor(out=osb[so][:, :, :], in0=osb[so][:, :, :],
                                                          in1=ssq[:, :, :].to_broadcast([128, 4, 256]), op=ALU.mult),
                         reads=[r_osb[so], r_ssq], writes=[r_osb[so]])
                    P.op("dve", lambda e: e.tensor_tensor(out=osf, in0=osf, in1=ngb[:, :], op=ALU.mult), reads=[r_osb[so], r_ng], writes=[r_osb[so]])
                    P.op("dve", lambda e: e.tensor_tensor(out=yb[:, :], in0=osf, in1=sr[:, :], op=ALU.mult), reads=[r_osb[so], r_sr], writes=[r_yb])
                    pendT.append(j)
                un += 1
            if last and int(os.environ.get("GLA_L2", "9")) >= 4:
                flushT()
                for dc in range(KC):
                    po_, ro_ = bank()
                    for fc in range(KC):
                        P.op("pe", lambda e: e.matmul(po_[:, :], lhsT=wos[:, fc, dc * 128:(dc + 1) * 128], rhs=yT[:, fc, :],
                                                      start=(fc == 0), stop=(fc == KC - 1)),
                             reads=[r_wo[fc]] + r_yT, writes=[ro_], signal=(fc == KC - 1))
                    P.op("dve", lambda e: e.tensor_tensor(out=hb[b][:, dc, :], in0=po_[:, :], in1=hb[b][:, dc, :], op=ALU.add),
                         reads=[ro_, r_hb[b][dc]], writes=[r_hb[b][dc]])
                P.dma("sp", C.dst[b], hv[:, :, i * TT:(i + 1) * TT], hb[b][:, :, :], reads=r_hb[b])
        P.barrier()

def _bf16(a):
    import ml_dtypes
    return np.ascontiguousarray(np.asarray(a, dtype=np.float32).astype(ml_dtypes.bfloat16))


def make_tables():
    T = {}
    a = np.arange(64)[:, None]
    c = np.arange(64)[None, :]
    al = 2 * np.pi * ((a * c) % 64) / 64.0
    T["F1"] = _bf16(np.concatenate([np.cos(al), np.sin(al), -np.sin(al)], axis=1))
    b = np.arange(128)[:, None, None]
    cc = np.arange(64)[None, :, None]
    d = np.arange(128)[None, None, :]
    ph = 2 * np.pi * ((b * (64 * d + cc)) % 8192) / 8192.0
    T["Mt"] = _bf16(np.stack([np.cos(ph), np.sin(ph)], axis=2))
    n2 = np.arange(256)[:, None]
    k2 = np.arange(256)[None, :]
    th = 2 * np.pi * ((n2 * k2) % 256) / 256.0
    cs = np.stack([np.cos(th), -np.sin(th)], axis=1)
    T["CS"] = _bf16(cs.reshape(2, 128, 2, 256).transpose(1, 0, 2, 3))
    slopes = np.exp2(-8.0 * np.arange(1, 25, dtype=np.float64) / 24.0)
    kk = np.arange(128)[:, None]
    qq = np.arange(128)[None, :]
    ET = np.zeros((128, 3, 2, 8, 128), np.float64)
    for g in range(3):
        for AB in range(2):
            rel = (kk - 64 + AB * 128) - qq
            valid = np.abs(rel) <= 64
            for h in range(8):
                ET[:, g, AB, h, :] = np.where(valid, np.exp(-slopes[g * 8 + h] * DIL[g] * np.abs(rel)), 0.0)
    T["EA"] = np.ascontiguousarray(ET.reshape(128, -1).astype(np.float32))
    T["ID"] = _bf16(np.eye(128))
    ss_ = np.arange(128)[:, None]
    tt_ = np.arange(128)[None, :]
    GLm = np.stack([(ss_ <= tt_), (ss_ >= tt_), (ss_ > tt_), (ss_ < tt_)], axis=1).astype(np.float64) * (-1.0 / 16.0)
    T["GL"] = np.ascontiguousarray(GLm.reshape(128, 512).astype(np.float32))
    m0 = (ss_ <= tt_).astype(np.float32)
    m1 = (ss_ > tt_).astype(np.float32)
    GM = np.stack([np.repeat(m0[:, None, :], 4, axis=1), np.repeat(m1[:, None, :], 4, axis=1)], axis=1)
    T["GM"] = np.ascontiguousarray(GM.reshape(128, 1024).astype(np.float32))
    return T


TABLE_SHAPES = {"F1": ([64, 192], BF16), "Mt": ([128, 64, 2, 128], BF16), "CS": ([128, 2, 2, 256], BF16),
                "EA": ([128, 3 * 2 * 8 * 128], F32), "ID": ([128, 128], BF16), "GL": ([128, 512], F32), "GM": ([128, 1024], F32)}

COLS = {
    "norm_g": 96, "final_norm_g": 8, "conv_b_pw1": 16, "conv_w_dw": 8 * 31, "conv_b_dw": 8, "conv_ln_g": 8, "conv_ln_b": 8,
    "conv_b_pw2": 8, "fnet_b": 8,
}


def make_cols(inp):
    def col(v):
        v = np.asarray(v, np.float32)
        lead = int(np.prod(v.shape[:-1])) if v.ndim > 1 else 1
        return np.ascontiguousarray(v.reshape(lead, -1, 128).transpose(2, 0, 1).reshape(128, -1))
    out = {}
    out["norm_g"] = col(inp["norm_g"])
    out["final_norm_g"] = col(inp["final_norm_g"])
    out["conv_b_pw1"] = col(inp["conv_b_pw1"][0])
    wdw = np.asarray(inp["conv_w_dw"][0], np.float32)
    out["conv_w_dw"] = np.ascontiguousarray(wdw.reshape(31, 8, 128).transpose(2, 1, 0).reshape(128, 8 * 31))
    for k in ("conv_b_dw", "conv_ln_g", "conv_ln_b", "conv_b_pw2", "fnet_b"):
        out[k] = col(inp[k][0])
    return out


WEIGHTS = {
    "pw1b": ("conv_w_pw1", (0,), [D, 2 * D]), "pw2b": ("conv_w_pw2", (0,), [D, D]), "wfb": ("fnet_w", (0,), [D, D]),
    "wqkvb": ("attn_w_qkv", (0,), [D, 4608]), "wob": ("attn_w_o", (0,), [512, D]),
    "winb": ("gla_w_in", (0,), [D, 3072]), "gwob": ("gla_w_o", (0,), [D, D]), "wa1b": ("gla_w_a1", (0,), [2, D, 16]),
}


def build_program(layers=(0, 1, 2, 3), do_ffn=True, do_mixer=True, do_final=True):
    nc = bass.Bass("TRN2", target_bir_lowering=False)
    C = Ctx()
    P = Prog(nc)
    xT = nc.dram_tensor("xT", [D, S], F32, kind="ExternalInput").ap()
    C.outT = nc.dram_tensor("outT", [D, S], F32, kind="ExternalOutput").ap()
    ext = {}
    ext["ffn_w1"] = nc.dram_tensor("ffn_w1", [4, 2, D, FF], F32, kind="ExternalInput").ap()
    ext["ffn_w3"] = nc.dram_tensor("ffn_w3", [4, 2, D, FF], F32, kind="ExternalInput").ap()
    ext["ffn_w2"] = nc.dram_tensor("ffn_w2", [4, 2, FF, D], F32, kind="ExternalInput").ap()
    ext["conv_w_pw1"] = nc.dram_tensor("conv_w_pw1", [1, D, 2 * D], F32, kind="ExternalInput").ap()
    ext["conv_w_pw2"] = nc.dram_tensor("conv_w_pw2", [1, D, D], F32, kind="ExternalInput").ap()
    ext["fnet_w"] = nc.dram_tensor("fnet_w", [1, D, D], F32, kind="ExternalInput").ap()
    ext["attn_w_qkv"] = nc.dram_tensor("attn_w_qkv", [1, D, 4608], F32, kind="ExternalInput").ap()
    ext["attn_w_o"] = nc.dram_tensor("attn_w_o", [1, 512, D], F32, kind="ExternalInput").ap()
    ext["gla_w_in"] = nc.dram_tensor("gla_w_in", [1, D, 3072], F32, kind="ExternalInput").ap()
    ext["gla_w_o"] = nc.dram_tensor("gla_w_o", [1, D, D], F32, kind="ExternalInput").ap()
    ext["gla_w_a1"] = nc.dram_tensor("gla_w_a1", [1, 2, D, 16], F32, kind="ExternalInput").ap()
    ext["gla_w_a2"] = nc.dram_tensor("gla_w_a2", [1, 2, 16, 512], F32, kind="ExternalInput").ap()
    ext["gla_b_a"] = nc.dram_tensor("gla_b_a", [1, 2, 512], F32, kind="ExternalInput").ap()
    C.ngd = nc.dram_tensor("c_gla_ng", [128, 1024], F32, kind="ExternalInput").ap()
    C.ext = ext
    colsd = {k: nc.dram_tensor("c_" + k, [128, n], F32, kind="ExternalInput").ap() for k, n in COLS.items()}
    C.tabs = {k: nc.dram_tensor("t_" + k, shp, dt, kind="ExternalInput").ap() for k, (shp, dt) in TABLE_SHAPES.items()}
    C.hs = nc.dram_tensor("hs", [D, S], F32).ap()
    C.GL = nc.dram_tensor("GL", [D, S + 32], BF16).ap()
    C.UT = nc.dram_tensor("UT", [D, S], BF16).ap()
    C.PT = nc.dram_tensor("PT", [D, S], BF16).ap()
    C.QT = nc.dram_tensor("QT", [D, S], BF16).ap()
    dk = "ExternalOutput" if os.environ.get("DBG") else "Internal"
    C.AQ = [nc.dram_tensor(f"AQ{g}", [512, S], BF16, kind=dk).ap() for g in range(3)]
    C.KT = [nc.dram_tensor(f"KT{g}", [512, S + 2 * PADK], BF16, kind=dk).ap() for g in range(3)]
    C.VA = [nc.dram_tensor(f"VA{g}", [S + 2 * PADK, 520], BF16, kind=dk).ap() for g in range(3)]
    C.NZ = [nc.dram_tensor(f"NZ{g}", [S, 520], F32, kind=dk).ap() for g in range(3)]
    C.OB = nc.dram_tensor("OB", [S, 1024], F32, kind=dk).ap()
    wb = {}
    for l in range(4):
        for j in range(2):
            wb[("w1", l, j)] = nc.dram_tensor(f"w1b_{l}_{j}", [D, FF], BF16).ap()
            wb[("w3", l, j)] = nc.dram_tensor(f"w3b_{l}_{j}", [D, FF], BF16).ap()
            wb[("w2", l, j)] = nc.dram_tensor(f"w2b_{l}_{j}", [FF, D], BF16).ap()
    W = {k: nc.dram_tensor(k, shp, BF16).ap() for k, (_, _, shp) in WEIGHTS.items()}
    C.ones_b = nc.alloc_sbuf_tensor("ones_b", [128, 128], BF16)
    C.ones_f = nc.alloc_sbuf_tensor("ones_f", [128, 128], F32)
    C.eps_col = nc.alloc_sbuf_tensor("eps_col", [128, 1], F32)
    C.one_col = nc.alloc_sbuf_tensor("one_col", [128, 1], F32)
    C.cols = {k: nc.alloc_sbuf_tensor("col_" + k, [128, n], F32) for k, n in COLS.items()}
    arena_bytes = nc.sbuf_bytes_remaining - 64
    arena = nc.alloc_sbuf_tensor("arena", [128, arena_bytes // 4], F32)
    base = nc.lookup_mloc(arena).addr
    C.sb = SB(nc, base, base + (arena_bytes // 4) * 4)
    C.ps = [nc.alloc_psum_tensor(f"ps{i}", [128, 512], F32) for i in range(6)]
    C.psS = nc.alloc_psum_tensor("psS", [128, 512], F32)
    C.dw = [P.dstream(f"w{i}") for i in range(12)]
    C.dwi = 0
    C.dh = [P.dstream(f"h{i}") for i in range(2)]
    C.dst = [P.dstream(f"st{i}") for i in range(2)]
    C.dm = [P.dstream(f"m{i}") for i in range(4)]
    dcast = [P.dstream(f"c{i}", barrier=False) for i in range(6)]
    r0 = P.res()
    P.op("dve", lambda e: e.memset(C.ones_b[:, :], 1.0), writes=[r0])
    P.op("dve", lambda e: e.memset(C.ones_f[:, :], 1.0), writes=[r0])
    P.op("dve", lambda e: e.memset(C.eps_col[:, :], EPS), writes=[r0])
    P.op("dve", lambda e: e.memset(C.one_col[:, :], 1.0), writes=[r0])
    for i, (k, n) in enumerate(COLS.items()):
        P.dma("sp", C.dm[i % 4], C.cols[k][:, :], colsd[k], writes=[r0])
    ci = 0
    C.cast_res = {}

    def cast(dst, src):
        nonlocal ci
        r = P.res()
        C.cast_res[dst.tensor.name] = r
        P.dma("pool", dcast[ci % len(dcast)], dst, src, writes=[r])
        ci += 1

    def cast_ffn(l, j):
        cast(wb[("w1", l, j)], ext["ffn_w1"][l, j])
        cast(wb[("w3", l, j)], ext["ffn_w3"][l, j])
        cast(wb[("w2", l, j)], ext["ffn_w2"][l, j])

    def cast_mixer(l):
        if l == 0:
            cast(W["wqkvb"], ext["attn_w_qkv"][0])
            cast(W["wob"], ext["attn_w_o"][0])
        if l == 1:
            cast(W["pw1b"], ext["conv_w_pw1"][0])
            cast(W["pw2b"], ext["conv_w_pw2"][0])
        if l == 2:
            cast(W["wfb"], ext["fnet_w"][0])
        if l == 3:
            cast(W["winb"], ext["gla_w_in"][0])
            cast(W["wa1b"], ext["gla_w_a1"][0])
            cast(W["gwob"], ext["gla_w_o"][0])

    gn = C.cols["norm_g"]
    phases = []
    for l in layers:
        if do_ffn:
            phases.append(("ffn", l, 0))
        if do_mixer:
            phases.append(("mix", l, 0))
        if do_ffn:
            phases.append(("ffn", l, 1))

    def do_cast(ph):
        if ph[0] == "ffn":
            cast_ffn(ph[1], ph[2])
        else:
            cast_mixer(ph[1])

    if phases:
        do_cast(phases[0])
    P.barrier()
    first = True
    for pi, ph in enumerate(phases):
        if pi + 1 < len(phases):
            do_cast(phases[pi + 1])
        kind, l, j = ph
        src = xT if first else None
        if kind == "ffn":
            gi = (l * 3 + (0 if j == 0 else 2)) * 8
            ffn_phase(P, C, wb[("w1", l, j)], wb[("w3", l, j)], wb[("w2", l, j)], gn[:, gi:gi + 8], hsrc=src)
        else:
            if first:
                P.dma("sp", C.dh[0], C.hs, xT, writes=[r0])
                P.barrier()
            g1 = gn[:, (l * 3 + 1) * 8:(l * 3 + 2) * 8]
            if l == 0:
                attn_phase(P, C, W, g1)
            elif l == 1:
                conv_phase(P, C, W, g1)
            elif l == 2:
                fourier_phase(P, C, W, g1)
            elif l == 3:
                gla_phase(P, C, W, g1)
        first = False
    if not phases:
        P.dma("sp", C.dh[0], C.hs, xT, writes=[r0])
        P.barrier()
    if do_final:
        final_phase(P, C, C.cols["final_norm_g"])
    else:
        P.dma("sp", C.dh[0], C.outT, C.hs)
    P.finish()
    C.P = P
    return nc, C


def make_in_maps(inp, ncores=NCORES):
    x = np.asarray(inp["x"], np.float32)
    cols = make_cols(inp)
    tabs = make_tables()
    shared = {}
    for k in ("ffn_w1", "ffn_w3", "ffn_w2", "conv_w_pw1", "conv_w_pw2", "fnet_w", "attn_w_qkv", "attn_w_o",
              "gla_w_in", "gla_w_o", "gla_w_a1", "gla_w_a2", "gla_b_a"):
        shared[k] = np.ascontiguousarray(np.asarray(inp[k], np.float32))
    for k, v in cols.items():
        shared["c_" + k] = v
    shared["c_gla_ng"] = np.ascontiguousarray(np.broadcast_to(np.asarray(inp["gla_norm_g"], np.float32).reshape(1, 1024), (128, 1024)))
    for k, v in tabs.items():
        shared["t_" + k] = v
    maps = []
    for b in range(ncores):
        m = dict(shared)
        m["xT"] = np.ascontiguousarray(x[b].T)
        maps.append(m)
    return maps


def kernel(**inputs):
    nc, C = build_program()
    in_maps = make_in_maps(inputs)
    res = run_bass_kernel_spmd(nc, in_maps, core_ids=list(range(NCORES)))
    out = np.stack([np.asarray(r["outT"]).T for r in res.results], axis=0)
    return np.ascontiguousarray(out.astype(np.float32))
```
